# Optimizing a Trainium2 kernel written in Bass

```python
import math
import jax, jax.numpy as jnp
from jax import lax
import numpy as np

D_MODEL = 2048
BATCH = 1
SEQ = 8192
DEPTH = 2

HEAD_DIM = 128
N_HEADS_A = D_MODEL // HEAD_DIM
N_KV_A = 4
ROT_DIM = HEAD_DIM // 4
IDX_HEADS = 16
IDX_DIM = 64
IDX_ROT = IDX_DIM // 4
TOPK_MAX = 256
N_HEADS_B = D_MODEL // HEAD_DIM
N_KV_B = 4
BLOCK_Q = 128
PLE_DIM = 256
ROPE_THETA = 500000.0
LN_EPS = 1e-5
N_A_LAYERS = DEPTH // 2
N_B_LAYERS = DEPTH - N_A_LAYERS
ALPHA = (2.0 * DEPTH) ** 0.25
BETA = (8.0 * DEPTH) ** -0.25

WIDTH_A = N_HEADS_A * HEAD_DIM
KVW_A = N_KV_A * HEAD_DIM
SPLITS_A = tuple(int(c) for c in np.cumsum([WIDTH_A, KVW_A, KVW_A, WIDTH_A, IDX_HEADS * IDX_DIM, IDX_DIM]))
COLS_A = WIDTH_A + 2 * KVW_A + WIDTH_A + IDX_HEADS * IDX_DIM + IDX_DIM + IDX_HEADS
WIDTH_B = N_HEADS_B * HEAD_DIM
KVW_B = N_KV_B * HEAD_DIM

kernel_name = "yoco_dsa_stickbreaking_deepnorm_ple"


def layer_norm(x, g, b):
    xf = x.astype(jnp.float32)
    mu = jnp.mean(xf, axis=-1, keepdims=True)
    var = jnp.mean(jnp.square(xf - mu), axis=-1, keepdims=True)
    y = (xf - mu) * lax.rsqrt(var + LN_EPS) * g.astype(jnp.float32) + b.astype(jnp.float32)
    return y.astype(x.dtype)


def partial_rope(x, positions, rot_dim):
    half = rot_dim // 2
    inv = 1.0 / (ROPE_THETA ** (jnp.arange(half, dtype=jnp.float32) / half))
    ang = positions.astype(jnp.float32)[..., None] * inv
    cos = jnp.cos(ang)[:, :, None, :]
    sin = jnp.sin(ang)[:, :, None, :]
    xr = x[..., :rot_dim].astype(jnp.float32)
    x1, x2 = xr[..., :half], xr[..., half:]
    rot = jnp.concatenate([x1 * cos - x2 * sin, x1 * sin + x2 * cos], axis=-1)
    return jnp.concatenate([rot.astype(x.dtype), x[..., rot_dim:]], axis=-1)


def to_blocks(a):
    b, s = a.shape[0], a.shape[1]
    return jnp.swapaxes(a.reshape((b, s // BLOCK_Q, BLOCK_Q) + a.shape[2:]), 0, 1)


def from_blocks(a):
    a = jnp.swapaxes(a, 0, 1)
    return a.reshape((a.shape[0], a.shape[1] * a.shape[2]) + a.shape[3:])


def mixer_a(x, w_in, w_o, positions):
    b, s, _ = x.shape
    proj = x @ w_in
    q, k, v, gate, iq, ik, iw = jnp.split(proj, SPLITS_A, axis=-1)
    q = partial_rope(q.reshape(b, s, N_HEADS_A, HEAD_DIM), positions, ROT_DIM)
    k = partial_rope(k.reshape(b, s, N_KV_A, HEAD_DIM), positions, ROT_DIM)
    v = v.reshape(b, s, N_KV_A, HEAD_DIM)
    iq = partial_rope(iq.reshape(b, s, IDX_HEADS, IDX_DIM), positions, IDX_ROT)
    ik = partial_rope(ik.reshape(b, s, 1, IDX_DIM), positions, IDX_ROT)[:, :, 0, :]
    n_sel = min(TOPK_MAX, s // 4)
    groups = N_HEADS_A // N_KV_A
    key_pos = jnp.arange(s, dtype=jnp.int32)
    idx_scale = (IDX_DIM ** -0.5) * (IDX_HEADS ** -0.5)
    starts = jnp.arange(s // BLOCK_Q, dtype=jnp.int32) * BLOCK_Q

    def block_fn(args):
        qb, iqb, iwb, start = args
        t = start + jnp.arange(BLOCK_Q, dtype=jnp.int32)
        dots = jnp.einsum('bqhd,bsd->bqhs', iqb, ik).astype(jnp.float32)
        iscore = jnp.einsum('bqhs,bqh->bqs', jax.nn.relu(dots), iwb.astype(jnp.float32)) * idx_scale
        causal = key_pos[None, :] <= t[:, None]
        iscore = jnp.where(causal[None], iscore, -jnp.inf)
        _, idx = lax.top_k(iscore, n_sel)
        ksel = jax.vmap(lambda kk, ii: kk[ii])(k, idx)
        vsel = jax.vmap(lambda vv, ii: vv[ii])(v, idx)
        qg = qb.reshape(b, BLOCK_Q, N_KV_A, groups, HEAD_DIM)
        logits = jnp.einsum('bqngd,bqknd->bqngk', qg, ksel).astype(jnp.float32) * (HEAD_DIM ** -0.5)
        valid = idx <= t[None, :, None]
        logits = jnp.where(valid[:, :, None, None, :], logits, -jnp.inf)
        wts = jax.nn.softmax(logits, axis=-1).astype(v.dtype)
        o = jnp.einsum('bqngk,bqknd->bqngd', wts, vsel)
        return o.reshape(b, BLOCK_Q, WIDTH_A)

    o = from_blocks(lax.map(block_fn, (to_blocks(q), to_blocks(iq), to_blocks(iw), starts)))
    return (o * jax.nn.silu(gate)) @ w_o


def shared_kv(x, w_kv):
    b, s, _ = x.shape
    k, v = jnp.split(x @ w_kv, [KVW_B], axis=-1)
    return k.reshape(b, s, N_KV_B, HEAD_DIM), v.reshape(b, s, N_KV_B, HEAD_DIM)


def mixer_b(x, w_in, w_o, k, v):
    b, s, _ = x.shape
    q, gate = jnp.split(x @ w_in, [WIDTH_B], axis=-1)
    q = q.reshape(b, s, N_HEADS_B, HEAD_DIM)
    groups = N_HEADS_B // N_KV_B
    key_pos = jnp.arange(s, dtype=jnp.int32)
    starts = jnp.arange(s // BLOCK_Q, dtype=jnp.int32) * BLOCK_Q

    def block_fn(args):
        qb, start = args
        t = start + jnp.arange(BLOCK_Q, dtype=jnp.int32)
        qg = qb.reshape(b, BLOCK_Q, N_KV_B, groups, HEAD_DIM)
        z = jnp.einsum('bqngd,bsnd->bngqs', qg, k).astype(jnp.float32) * (HEAD_DIM ** -0.5)
        strict = key_pos[None, :] < t[:, None]
        log_om = jnp.where(strict, -jax.nn.softplus(z), 0.0)
        suffix = lax.cumsum(log_om, axis=log_om.ndim - 1, reverse=True) - log_om
        a = jnp.where(strict, jnp.exp(jax.nn.log_sigmoid(z) + suffix), 0.0)
        o = jnp.einsum('bngqs,bsnd->bqngd', a.astype(v.dtype), v)
        return o.reshape(b, BLOCK_Q, WIDTH_B)

    o = from_blocks(lax.map(block_fn, (to_blocks(q), starts)))
    return (o * jax.nn.silu(gate)) @ w_o


def per_layer_embed(x, p_i, w_p, w_g):
    return (p_i @ w_p) * jax.nn.sigmoid(x @ w_g)


def setup_inputs(seed: int = 0) -> dict:
    key = jax.random.key(seed)
    ks = jax.random.split(key, 12)
    nrm = jax.random.normal
    x = nrm(ks[0], (BATCH, SEQ, D_MODEL), jnp.float32)
    p = nrm(ks[1], (DEPTH, BATCH, SEQ, PLE_DIM), jnp.float32)
    positions = jnp.broadcast_to(jnp.arange(SEQ, dtype=jnp.int32)[None, :], (BATCH, SEQ))
    w_in_a = nrm(ks[2], (N_A_LAYERS, D_MODEL, COLS_A), jnp.float32) * D_MODEL ** -0.5
    w_o_a = nrm(ks[3], (N_A_LAYERS, WIDTH_A, D_MODEL), jnp.float32) * (WIDTH_A ** -0.5 * BETA)
    w_kv_b = nrm(ks[4], (D_MODEL, 2 * KVW_B), jnp.float32) * D_MODEL ** -0.5
    w_in_b = nrm(ks[5], (N_B_LAYERS, D_MODEL, 2 * WIDTH_B), jnp.float32) * D_MODEL ** -0.5
    w_o_b = nrm(ks[6], (N_B_LAYERS, WIDTH_B, D_MODEL), jnp.float32) * (WIDTH_B ** -0.5 * BETA)
    ln_g = 1.0 + 0.02 * nrm(ks[7], (DEPTH, D_MODEL), jnp.float32)
    ln_b = 0.02 * nrm(ks[8], (DEPTH, D_MODEL), jnp.float32)
    w_ple = nrm(ks[9], (DEPTH, PLE_DIM, D_MODEL), jnp.float32) * PLE_DIM ** -0.5
    w_ple_gate = nrm(ks[10], (DEPTH, D_MODEL, D_MODEL), jnp.float32) * D_MODEL ** -0.5
    return {"x": x, "p": p, "positions": positions, "w_in_a": w_in_a, "w_o_a": w_o_a,
            "w_kv_b": w_kv_b, "w_in_b": w_in_b, "w_o_b": w_o_b, "ln_g": ln_g, "ln_b": ln_b,
            "w_ple": w_ple, "w_ple_gate": w_ple_gate}


def reference(x, p, positions, w_in_a, w_o_a, w_kv_b, w_in_b, w_o_b, ln_g, ln_b, w_ple, w_ple_gate):
    k_sh, v_sh = None, None
    for i in range(DEPTH):
        if i < N_A_LAYERS:
            h = mixer_a(x, w_in_a[i], w_o_a[i], positions)
        else:
            if i == N_A_LAYERS:
                k_sh, v_sh = shared_kv(x, w_kv_b)
            j = i - N_A_LAYERS
            h = mixer_b(x, w_in_b[j], w_o_b[j], k_sh, v_sh)
        x = layer_norm(ALPHA * x + h, ln_g[i], ln_b[i])
        x = x + per_layer_embed(x, p[i], w_ple[i], w_ple_gate[i])
    return x
```

```python
import numpy as np
from contextlib import ExitStack
import concourse.bass as bass
import concourse.mybir as mybir
from concourse.bass_utils import run_bass_kernel_spmd
import ml_dtypes

F32 = mybir.dt.float32
BF16 = mybir.dt.bfloat16
I32 = mybir.dt.int32
AF = mybir.ActivationFunctionType
ALU = mybir.AluOpType
AX = mybir.AxisListType


class _Ctr:
    def __init__(self, sem, name):
        self.sem = sem
        self.count = 0
        self.name = name


class _Res:
    def __init__(self, name, t=None):
        self.name = name
        self.t = t
        self.lw = None
        self.rd = {}
        self.dsem = None


class _Eng:
    def __init__(self, name, h, ctr):
        self.name = name
        self.h = h
        self.ctr = ctr
        self.seen = {}


class Sched:
    def __init__(self, nc):
        self.nc = nc
        self.es = ExitStack()
        self.eng = {}
        for name, h in (("pe", nc.tensor), ("act", nc.scalar), ("dve", nc.vector),
                        ("pool", nc.gpsimd), ("sp", nc.sync)):
            sem = self.es.enter_context(nc.semaphore("sem_" + name))
            self.eng[name] = _Eng(name, h, _Ctr(sem, name))
        self.dma_res = []

    def sbuf(self, name, shape, dtype):
        t = self.es.enter_context(self.nc.sbuf_tensor("sb_" + name, shape, dtype))
        return _Res(name, t)

    def psum(self, name, shape, dtype):
        t = self.es.enter_context(self.nc.psum_tensor("pp_" + name, shape, dtype))
        return _Res(name, t)

    def dram(self, ap, name):
        return _Res(name, ap)

    def view(self, name):
        return _Res(name, None)

    def _dsem(self, r):
        if r.dsem is None:
            sem = self.es.enter_context(self.nc.semaphore("dsem_" + r.name))
            r.dsem = _Ctr(sem, r.name)
            self.dma_res.append(r)
        return r.dsem

    def _waits(self, E, reads, writes):
        need = {}

        def req(kv, raw):
            if kv is None:
                return
            key, val = kv
            if key is E.ctr:
                if not raw or E.name in ("pe", "sp"):
                    return
            if need.get(key, 0) < val:
                need[key] = val

        for r in reads:
            req(r.lw, True)
        for w in writes:
            req(w.lw, False)
            for k, v in w.rd.items():
                req((k, v), False)
        for key, val in need.items():
            if E.seen.get(key, 0) >= val:
                continue
            E.h.wait_ge(key.sem, val)
            E.seen[key] = val

    def op(self, eng, fn, reads=(), writes=()):
        E = self.eng[eng]
        self._waits(E, reads, writes)
        inst = fn()
        E.ctr.count += 1
        inst.then_inc(E.ctr.sem, 1)
        for w in writes:
            w.lw = (E.ctr, E.ctr.count)
            w.rd = {}
        for r in reads:
            if r not in writes:
                r.rd[E.ctr] = E.ctr.count
        return inst

    def dma(self, queue, dst, out_ap, in_ap, reads=(), cast=False):
        E = self.eng[queue]
        self._waits(E, reads, [dst])
        c = self._dsem(dst)
        inst = E.h.dma_start(out=out_ap, in_=in_ap)
        inst.then_inc(c.sem, 16)
        c.count += 16
        dst.lw = (c, c.count)
        dst.rd = {}
        for r in reads:
            r.rd[c] = c.count
        return inst

    def collective(self, kind, dst, out_ap, in_ap, reads=()):
        E = self.eng["pool"]
        self._waits(E, reads, [dst])
        c = self._dsem(dst)
        inst = E.h.collective_compute(kind, ALU.bypass, replica_groups=[list(range(NCORES))], ins=[in_ap], outs=[out_ap])
        inst.then_inc(c.sem, 16)
        c.count += 16
        dst.lw = (c, c.count)
        dst.rd = {}
        for r in reads:
            r.rd[c] = c.count
        return inst

    def finish(self):
        sp = self.eng["sp"]
        for r in self.dma_res:
            if r.dsem.count > 0 and sp.seen.get(r.dsem, 0) < r.dsem.count:
                sp.h.wait_ge(r.dsem.sem, r.dsem.count)
        for name in ("pe", "act", "dve", "pool"):
            c = self.eng[name].ctr
            if c.count > 0:
                sp.h.wait_ge(c.sem, c.count)
        self.es.close()


NCORES = 8
NBL = 8
TOK = 1024
DM = 2048
SEQ = 8192
THETA = 500000.0
ALPHA = 4.0 ** 0.25
LN_EPS = 1e-5
NEG = -1.0e30
PI = float(np.pi)
C1 = 6.28125
C2 = float(2.0 * np.pi - 6.28125)
NBIS = 24


def blk(c, j):
    return 16 * (j // 2) + (c if j % 2 == 0 else 15 - c)


def _rr(lst, i):
    return lst[i % len(lst)]


def emit_rope_tables(S, nc, pos_ap, ropec, ci, cosT, sinT, tmp):
    posi, posf, ang, kf, ki, r, m = tmp
    S.dma("sp", posi, posi.t[:], pos_ap.partition_broadcast(128))
    S.op("dve", lambda: nc.vector.tensor_copy(posf.t[:], posi.t[:]), reads=[posi], writes=[posf])
    for which, dst in ((0, sinT), (1, cosT)):
        if which == 0:
            S.op("dve", lambda: nc.vector.tensor_scalar(ang.t[:], posf.t[:], ropec.t[:, ci:ci + 1], None, op0=ALU.mult),
                 reads=[posf, ropec], writes=[ang])
        else:
            S.op("dve", lambda: nc.vector.tensor_scalar(ang.t[:], posf.t[:], ropec.t[:, ci:ci + 1], PI / 2, op0=ALU.mult, op1=ALU.add),
                 reads=[posf, ropec], writes=[ang])
        S.op("dve", lambda: nc.vector.tensor_scalar(ki.t[:], ang.t[:], 1.0 / (2 * PI), None, op0=ALU.mult), reads=[ang], writes=[ki])
        S.op("dve", lambda: nc.vector.tensor_copy(kf.t[:], ki.t[:]), reads=[ki], writes=[kf])
        S.op("dve", lambda: nc.vector.scalar_tensor_tensor(r.t[:], kf.t[:], -C1, ang.t[:], op0=ALU.mult, op1=ALU.add),
             reads=[kf, ang], writes=[r])
        S.op("dve", lambda: nc.vector.scalar_tensor_tensor(ang.t[:], kf.t[:], -C2, r.t[:], op0=ALU.mult, op1=ALU.add),
             reads=[kf, r], writes=[ang])
        S.op("dve", lambda: nc.vector.tensor_scalar(m.t[:], ang.t[:], PI, -2 * PI, op0=ALU.is_gt, op1=ALU.mult), reads=[ang], writes=[m])
        S.op("dve", lambda: nc.vector.tensor_tensor(r.t[:], ang.t[:], m.t[:], op=ALU.add), reads=[ang, m], writes=[r])
        S.op("dve", lambda: nc.vector.tensor_scalar(m.t[:], r.t[:], -PI, 2 * PI, op0=ALU.is_lt, op1=ALU.mult), reads=[r], writes=[m])
        S.op("dve", lambda: nc.vector.tensor_tensor(ang.t[:], r.t[:], m.t[:], op=ALU.add), reads=[r, m], writes=[ang])
        S.op("dve", lambda: nc.vector.tensor_scalar(r.t[:], ang.t[:], PI, -PI, op0=ALU.min, op1=ALU.max), reads=[ang], writes=[r])
        S.op("act", lambda: nc.scalar.activation(dst.t[:], r.t[:], AF.Sin), reads=[r], writes=[dst])
        if which == 0:
            S.op("dve", lambda: nc.vector.tensor_scalar(dst.t[:], dst.t[:], ropec.t[:, ci + 1:ci + 2], None, op0=ALU.mult),
                 reads=[dst, ropec], writes=[dst])


def emit_proj(S, nc, xb, w_ap, chunks, v_spec, iw_spec, rope, wbufs, stg, tmpf, psb):
    ngroups = (len(chunks) + 3) // 4
    cnt = 0
    for g in range(ngroups):
        wb = _rr(wbufs, g)
        gch = chunks[4 * g:4 * g + 4]
        ncol = 128 * len(gch)
        S.dma("pool", wb, wb.t[:, :, 0:ncol], w_ap[:, 512 * g:512 * g + ncol].rearrange("(c p) n -> p c n", p=128))
        for ci, (kind, oap, ores) in enumerate(gch):
            for hh in range(2):
                ps = _rr(psb, cnt)
                st = _rr(stg, cnt)
                cnt += 1

                def mm(ps=ps, wb=wb, ci=ci, hh=hh):
                    last = None
                    for k in range(16):
                        last = nc.tensor.matmul(ps.t[:], wb.t[:, k, ci * 128:(ci + 1) * 128], xb.t[:, k, hh * 512:(hh + 1) * 512],
                                                start=(k == 0), stop=(k == 15))
                    return last
                S.op("pe", mm, reads=[wb, xb], writes=[ps])
                sl = slice(hh * 512, (hh + 1) * 512)
                if kind == "g":
                    S.op("act", lambda ps=ps, st=st: nc.scalar.activation(st.t[:], ps.t[:], AF.Silu), reads=[ps], writes=[st])
                else:
                    sc = (128.0 ** -0.5) if kind in ("q", "q1") else 1.0
                    S.op("act", lambda ps=ps, st=st, sc=sc: nc.scalar.activation(st.t[:], ps.t[:], AF.Copy, scale=sc), reads=[ps], writes=[st])
                    if kind in ("q", "k", "iq"):
                        perm, cosT, sinT = rope["qk"] if kind in ("q", "k") else rope["i"]
                        ps2 = _rr(psb, cnt)
                        cnt += 1
                        t1, t2 = tmpf
                        S.op("pe", lambda ps2=ps2, st=st, perm=perm: nc.tensor.matmul(ps2.t[:], perm.t[:], st.t[:], start=True, stop=True),
                             reads=[perm, st], writes=[ps2])
                        S.op("dve", lambda st=st, cosT=cosT, sl=sl: nc.vector.tensor_tensor(t1.t[:], st.t[:], cosT.t[:, sl], op=ALU.mult),
                             reads=[st, cosT], writes=[t1])
                        S.op("dve", lambda ps2=ps2, sinT=sinT, sl=sl: nc.vector.tensor_tensor(t2.t[:], ps2.t[:], sinT.t[:, sl], op=ALU.mult),
                             reads=[ps2, sinT], writes=[t2])
                        S.op("dve", lambda st=st: nc.vector.tensor_tensor(st.t[:], t1.t[:], t2.t[:], op=ALU.add), reads=[t1, t2], writes=[st])
                S.dma("sp", ores, oap[:, sl], st.t[:], reads=[st])
    for spec, width in ((v_spec, 512), (iw_spec, 16)):
        if spec is None:
            continue
        col0, oap, ores, odt = spec
        wb = _rr(wbufs, ngroups + (0 if width == 512 else 1))
        S.dma("pool", wb, wb.t[:, :, 0:width], w_ap[:, col0:col0 + width].rearrange("(c p) n -> p c n", p=128))
        for tt in range(8):
            ps = _rr(psb, cnt)
            cnt += 1

            def mm(ps=ps, wb=wb, tt=tt, width=width):
                last = None
                for k in range(16):
                    last = nc.tensor.matmul(ps.t[:, 0:width], xb.t[:, k, tt * 128:(tt + 1) * 128], wb.t[:, k, 0:width],
                                            start=(k == 0), stop=(k == 15))
                return last
            S.op("pe", mm, reads=[wb, xb], writes=[ps])
            if width == 512:
                st = _rr(stg, cnt)
                S.op("act", lambda ps=ps, st=st: nc.scalar.activation(st.t[:], ps.t[:], AF.Copy), reads=[ps], writes=[st])
                S.dma("sp", ores, oap[tt * 128:(tt + 1) * 128, :], st.t[:], reads=[st])
            else:
                t1 = tmpf[0]
                S.op("act", lambda ps=ps, t1=t1: nc.scalar.activation(t1.t[:, 0:16], ps.t[:, 0:16], AF.Copy), reads=[ps], writes=[t1])
                S.dma("sp", ores, oap[tt * 128:(tt + 1) * 128, :], t1.t[:, 0:16], reads=[t1])


def build_A():
    nc = bass.Bass("TRN2", target_bir_lowering=False)
    WCOLS = 45 * 128 + 512 + 16
    xT = nc.dram_tensor("xT", [DM, TOK], F32, kind="ExternalInput").ap()
    pos = nc.dram_tensor("pos", [1, TOK], I32, kind="ExternalInput").ap()
    w = nc.dram_tensor("w", [DM, WCOLS], F32, kind="ExternalInput").ap()
    ropec_d = nc.dram_tensor("ropec", [128, 4], F32, kind="ExternalInput").ap()
    pqk_d = nc.dram_tensor("perm_qk", [128, 128], BF16, kind="ExternalInput").ap()
    pi_d = nc.dram_tensor("perm_i", [128, 128], BF16, kind="ExternalInput").ap()
    QT = nc.dram_tensor("QT", [16, 128, TOK], BF16, kind="ExternalOutput").ap()
    KT = nc.dram_tensor("KT", [4, 128, TOK], BF16, kind="ExternalOutput").ap()
    SG = nc.dram_tensor("SG", [16, 128, TOK], BF16, kind="ExternalOutput").ap()
    IQT = nc.dram_tensor("IQT", [8, 128, TOK], BF16, kind="ExternalOutput").ap()
    IKT = nc.dram_tensor("IKT", [128, TOK], BF16, kind="ExternalOutput").ap()
    V = nc.dram_tensor("V", [TOK, 512], BF16, kind="ExternalOutput").ap()
    IW = nc.dram_tensor("IW", [TOK, 16], F32, kind="ExternalOutput").ap()
    S = Sched(nc)
    xb = S.sbuf("xb", [128, 16, TOK], BF16)
    S.dma("pool", xb, xb.t[:], xT.rearrange("(c p) t -> p c t", p=128))
    ropec = S.sbuf("ropec", [128, 4], F32)
    pqk = S.sbuf("pqk", [128, 128], BF16)
    pii = S.sbuf("pii", [128, 128], BF16)
    S.dma("sp", ropec, ropec.t[:], ropec_d)
    S.dma("sp", pqk, pqk.t[:], pqk_d)
    S.dma("sp", pii, pii.t[:], pi_d)
    tabs = [S.sbuf("tab%d" % i, [128, TOK], F32) for i in range(4)]
    posi = S.sbuf("posi", [128, TOK], I32)
    ki = S.sbuf("ki", [128, TOK], I32)
    tmp = [posi] + [S.sbuf("rt%d" % i, [128, TOK], F32) for i in range(3)] + [ki] + [S.sbuf("rt%d" % i, [128, TOK], F32) for i in range(3, 5)]
    emit_rope_tables(S, nc, pos, ropec, 0, tabs[0], tabs[1], tmp)
    emit_rope_tables(S, nc, pos, ropec, 2, tabs[2], tabs[3], tmp)
    rope = {"qk": (pqk, tabs[0], tabs[1]), "i": (pii, tabs[2], tabs[3])}
    wbufs = [S.sbuf("wb%d" % i, [128, 16, 512], BF16) for i in range(2)]
    stg = [S.sbuf("stg%d" % i, [128, 512], BF16) for i in range(4)]
    tmpf = [S.sbuf("tmpf%d" % i, [128, 512], F32) for i in range(2)]
    psb = [S.psum("ps%d" % i, [128, 512], F32) for i in range(4)]
    rq, rk, rs, riq, rik, rv, riw = [S.dram(a, n) for a, n in ((QT, "QT"), (KT, "KT"), (SG, "SG"), (IQT, "IQT"), (IKT, "IKT"), (V, "V"), (IW, "IW"))]
    chunks = [("q", QT[h], rq) for h in range(16)] + [("k", KT[n], rk) for n in range(4)] + \
             [("g", SG[h], rs) for h in range(16)] + [("iq", IQT[h], riq) for h in range(8)] + [("iq", IKT, rik)]
    emit_proj(S, nc, xb, w, chunks, (45 * 128, V, rv, BF16), (45 * 128 + 512, IW, riw, F32), rope, wbufs, stg, tmpf, psb)
    S.finish()
    return nc


def _bf(a):
    return np.asarray(a, dtype=np.float32).astype(ml_dtypes.bfloat16)


def host_consts():
    p = np.arange(128)
    ropec = np.zeros((128, 4), np.float32)
    inv_qk = (np.float32(1.0) / np.power(np.float32(THETA), np.arange(16, dtype=np.float32) / np.float32(16))).astype(np.float32)
    inv_i = (np.float32(1.0) / np.power(np.float32(THETA), np.arange(8, dtype=np.float32) / np.float32(8))).astype(np.float32)
    for q in range(128):
        if q < 32:
            ropec[q, 0] = inv_qk[q % 16]
            ropec[q, 1] = -1.0 if q < 16 else 1.0
        r = q % 64
        if r < 16:
            ropec[q, 2] = inv_i[r % 8]
            ropec[q, 3] = -1.0 if r < 8 else 1.0
    pqk = np.zeros((128, 128), np.float32)
    pii = np.zeros((128, 128), np.float32)
    for m in range(128):
        pm = m + 16 if m < 16 else (m - 16 if m < 32 else m)
        pqk[pm, m] = 1.0
        r = m % 64
        pm = m + 8 if r < 8 else (m - 8 if r < 16 else m)
        pii[pm, m] = 1.0
    ident = np.eye(128, dtype=np.float32)
    utri = (p[:, None] >= p[None, :]).astype(np.float32)
    return {"ropec": ropec, "perm_qk": _bf(pqk), "perm_i": _bf(pii), "ident": _bf(ident),
            "ones_bf": _bf(np.ones((128, 128))), "ones_f": np.ones((128, 128), np.float32), "utri": _bf(utri)}


def host_tokens(c):
    return np.concatenate([np.arange(128 * blk(c, j), 128 * blk(c, j) + 128) for j in range(NBL)])


def host_wA(w_in_a):
    w = w_in_a[0]
    q, k, v, gate, iq, ik, iw = np.split(w, [2048, 2560, 3072, 5120, 6144, 6208], axis=1)
    return np.ascontiguousarray(np.concatenate([q, k, gate, iq, ik, ik, v, iw], axis=1))


def build_B():
    nc = bass.Bass("TRN2", target_bir_lowering=False)
    QTJ = nc.dram_tensor("QTJ", [NBL, 128, 16, 128], BF16, kind="ExternalInput").ap()
    SGJ = nc.dram_tensor("SGJ", [NBL, 128, 16, 128], BF16, kind="ExternalInput").ap()
    IQJ = nc.dram_tensor("IQJ", [NBL, 128, 8, 128], BF16, kind="ExternalInput").ap()
    IWd = nc.dram_tensor("IW", [TOK, 16], F32, kind="ExternalInput").ap()
    IKA = nc.dram_tensor("IKA", [128, SEQ], BF16, kind="ExternalInput").ap()
    KTA = nc.dram_tensor("KTA", [4, 128, SEQ], BF16, kind="ExternalInput").ap()
    VH = nc.dram_tensor("VH", [4, 128, 64, 128], BF16, kind="ExternalInput").ap()
    PEN = nc.dram_tensor("PEN", [128, NBL, 1024], F32, kind="ExternalInput").ap()
    ident_d = nc.dram_tensor("ident", [128, 128], BF16, kind="ExternalInput").ap()
    ones_d = nc.dram_tensor("ones_bf", [128, 128], BF16, kind="ExternalInput").ap()
    OGT = nc.dram_tensor("OGT", [16, 128, TOK], BF16, kind="ExternalOutput").ap()
    S = Sched(nc)
    ident = S.sbuf("ident", [128, 128], BF16)
    ones = S.sbuf("ones", [128, 128], BF16)
    ikt = S.sbuf("ikt", [128, SEQ], BF16)
    S.dma("sp", ident, ident.t[:], ident_d)
    S.dma("sp", ones, ones.t[:], ones_d)
    S.dma("sp", ikt, ikt.t[:], IKA)
    qj = S.sbuf("qj", [128, 16, 128], BF16)
    sgj = S.sbuf("sgj", [128, 16, 128], BF16)
    iqj = S.sbuf("iqj", [128, 8, 128], BF16)
    iwj = S.sbuf("iwj", [128, 16], F32)
    penj = S.sbuf("penj", [128, 1024], F32)
    absw = S.sbuf("absw", [128, 16], F32)
    sgn = S.sbuf("sgn", [128, 16], F32)
    Dg = S.sbuf("Dg", [128, 16, 128], BF16)
    isc = S.sbuf("isc", [128, SEQ], F32)
    Mb = S.sbuf("Mb", [128, SEQ], BF16)
    MT = S.sbuf("MT", [128, SEQ], BF16)
    pw = S.sbuf("pw", [128, NBIS], F32)
    halfs = S.sbuf("halfs", [128, NBIS], F32)
    lo = S.sbuf("lo", [128, 1], F32)
    hi = S.sbuf("hi", [128, 1], F32)
    rng = S.sbuf("rng", [128, 1], F32)
    mid = S.sbuf("mid", [128, 1], F32)
    cntt = S.sbuf("cntt", [128, 1], F32)
    step = S.sbuf("step", [128, 1], F32)
    rbs = [S.sbuf("rb%d" % i, [128, 512], BF16) for i in range(4)]
    pts = [S.sbuf("pt%d" % i, [128, 4, 128], BF16) for i in range(3)]
    pms = [S.sbuf("pm%d" % i, [128, 4, 128], BF16) for i in range(3)]
    kbufs = [S.sbuf("kbuf%d" % i, [128, 1024], BF16) for i in range(3)]
    vbufs = [S.sbuf("vbuf%d" % i, [128, 8, 128], BF16) for i in range(3)]
    rden = S.sbuf("rden", [128, 512], F32)
    onrm = S.sbuf("onrm", [128, 4, 128], F32)
    ogt = S.sbuf("ogt", [128, 4, 128], BF16)
    W = [S.psum("W%d" % i, [128, 512], F32) for i in range(3)]
    ISC = S.psum("ISC", [128, 512], F32)
    TP = S.psum("TP", [128, 1024], BF16)
    OP = S.psum("OP", [128, 512], F32)
    DEN = S.psum("DEN", [128, 512], F32)
    rOGT = S.dram(OGT, "OGT")
    for k in range(NBIS):
        S.op("dve", lambda k=k: nc.vector.memset(pw.t[:, k:k + 1], 2.0 ** -(k + 1)), writes=[pw])
    wc = 0
    ec = 0
    kvc = 0
    for j in range(NBL):
        Sj = 1024 * (j + 1)
        S.dma("sp", qj, qj.t[:], QTJ[j])
        S.dma("sp", sgj, sgj.t[:], SGJ[j])
        S.dma("sp", iqj, iqj.t[:], IQJ[j])
        S.dma("sp", iwj, iwj.t[:], IWd[j * 128:(j + 1) * 128, :])
        S.dma("sp", penj, penj.t[:], PEN[:, j, :])
        S.op("act", lambda: nc.scalar.activation(absw.t[:], iwj.t[:], AF.Abs), reads=[iwj], writes=[absw])
        S.op("dve", lambda: nc.vector.tensor_scalar(sgn.t[:], iwj.t[:], 0.0, 2.0, op0=ALU.is_ge, op1=ALU.mult), reads=[iwj], writes=[sgn])
        S.op("dve", lambda: nc.vector.tensor_scalar(sgn.t[:], sgn.t[:], -1.0, None, op0=ALU.add), reads=[sgn], writes=[sgn])
        for h in range(16):
            S.op("dve", lambda h=h: nc.vector.tensor_scalar(Dg.t[:, h, :], ident.t[:], sgn.t[:, h:h + 1], None, op0=ALU.mult),
                 reads=[ident, sgn], writes=[Dg])
        for c in range(Sj // 512):
            for h in range(16):
                pb = 64 * (h % 2)
                d = _rr(W, wc)
                wc += 1
                rb = _rr(rbs, ec)
                S.op("pe", lambda d=d, pb=pb, h=h, c=c: nc.tensor.matmul(d.t[:], iqj.t[pb:pb + 64, h // 2, :], ikt.t[pb:pb + 64, c * 512:(c + 1) * 512],
                                                                     start=True, stop=True), reads=[iqj, ikt], writes=[d])
                if ec % 2 == 0:
                    S.op("act", lambda d=d, rb=rb, h=h: nc.scalar.activation(rb.t[:], d.t[:], AF.Relu, scale=absw.t[:, h:h + 1]),
                         reads=[d, absw], writes=[rb])
                else:
                    S.op("dve", lambda d=d, rb=rb, h=h: nc.vector.tensor_scalar(rb.t[:], d.t[:], 0.0, absw.t[:, h:h + 1], op0=ALU.max, op1=ALU.mult),
                         reads=[d, absw], writes=[rb])
                ec += 1
                S.op("pe", lambda rb=rb, h=h: nc.tensor.matmul(ISC.t[:], Dg.t[:, h, :], rb.t[:], start=(h == 0), stop=(h == 15)),
                     reads=[Dg, rb], writes=[ISC])
            S.op("act", lambda c=c: nc.scalar.activation(isc.t[:, c * 512:(c + 1) * 512], ISC.t[:], AF.Copy), reads=[ISC], writes=[isc])
        S.op("dve", lambda: nc.vector.tensor_reduce(lo.t[:], isc.t[:, 0:Sj], axis=AX.X, op=ALU.min), reads=[isc], writes=[lo])
        S.op("dve", lambda: nc.vector.tensor_tensor(isc.t[:, Sj - 1024:Sj], isc.t[:, Sj - 1024:Sj], penj.t[:], op=ALU.add), reads=[isc, penj], writes=[isc])
        S.op("dve", lambda: nc.vector.tensor_reduce(hi.t[:], isc.t[:, 0:Sj], axis=AX.X, op=ALU.max), reads=[isc], writes=[hi])
        S.op("dve", lambda: nc.vector.tensor_tensor(rng.t[:], hi.t[:], lo.t[:], op=ALU.subtract), reads=[hi, lo], writes=[rng])
        S.op("dve", lambda: nc.vector.tensor_scalar(halfs.t[:], pw.t[:], rng.t[:, 0:1], None, op0=ALU.mult), reads=[pw, rng], writes=[halfs])
        for k in range(NBIS):
            S.op("dve", lambda k=k: nc.vector.tensor_tensor(mid.t[:], lo.t[:], halfs.t[:, k:k + 1], op=ALU.add), reads=[lo, halfs], writes=[mid])
            S.op("dve", lambda: nc.vector.tensor_scalar(Mb.t[:, 0:Sj], isc.t[:, 0:Sj], mid.t[:, 0:1], None, op0=ALU.is_ge, op1=ALU.add, accum_out=cntt.t[:]),
                 reads=[isc, mid], writes=[Mb, cntt])
            S.op("dve", lambda k=k: nc.vector.scalar_tensor_tensor(step.t[:], cntt.t[:], 255.5, halfs.t[:, k:k + 1], op0=ALU.is_ge, op1=ALU.mult),
                 reads=[cntt, halfs], writes=[step])
            S.op("dve", lambda: nc.vector.tensor_tensor(lo.t[:], lo.t[:], step.t[:], op=ALU.add), reads=[lo, step], writes=[lo])
        S.op("dve", lambda: nc.vector.tensor_scalar(Mb.t[:, 0:Sj], isc.t[:, 0:Sj], lo.t[:, 0:1], None, op0=ALU.is_ge), reads=[isc, lo], writes=[Mb])
        for g in range(Sj // 1024):
            for r in range(8):
                kb = 8 * g + r
                S.op("pe", lambda r=r, kb=kb: nc.tensor.transpose(TP.t[:, r * 128:(r + 1) * 128], Mb.t[:, kb * 128:(kb + 1) * 128], ident.t[:]),
                     reads=[Mb, ident], writes=[TP])
            if g % 2 == 0:
                S.op("act", lambda g=g: nc.scalar.activation(MT.t[:, g * 1024:(g + 1) * 1024], TP.t[:], AF.Copy), reads=[TP], writes=[MT])
            else:
                S.op("dve", lambda g=g: nc.vector.tensor_copy(MT.t[:, g * 1024:(g + 1) * 1024], TP.t[:]), reads=[TP], writes=[MT])
        nkb = Sj // 128
        for n in range(4):
            for g in range(j + 1):
                kbuf = _rr(kbufs, kvc)
                vbuf = _rr(vbufs, kvc)
                kvc += 1
                S.dma("sp", kbuf, kbuf.t[:], KTA[n, :, g * 1024:(g + 1) * 1024])
                S.dma("sp", vbuf, vbuf.t[:], VH[n, :, 8 * g:8 * g + 8, :])
                for r in range(8):
                    kb = 8 * g + r
                    st = _rr(W, wc)
                    pt = _rr(pts, wc)
                    pm = _rr(pms, wc)
                    wc += 1
                    S.op("pe", lambda st=st, kbuf=kbuf, r=r, n=n: nc.tensor.matmul(st.t[:], kbuf.t[:, r * 128:(r + 1) * 128], qj.t[:, 4 * n:4 * n + 4, :],
                                                                              start=True, stop=True), reads=[kbuf, qj], writes=[st])
                    S.op("act", lambda st=st, pt=pt: nc.scalar.activation(pt.t[:], st.t[:].rearrange("p (h t) -> p h t", h=4), AF.Exp),
                         reads=[st], writes=[pt])
                    S.op("dve", lambda pt=pt, pm=pm, kb=kb: nc.vector.tensor_tensor(
                        pm.t[:], pt.t[:], MT.t[:, kb * 128:(kb + 1) * 128].unsqueeze(1).to_broadcast([128, 4, 128]), op=ALU.mult),
                        reads=[pt, MT], writes=[pm])
                    S.op("pe", lambda pm=pm, vbuf=vbuf, r=r, kb=kb: nc.tensor.matmul(OP.t[:], vbuf.t[:, r, :], pm.t[:].rearrange("p h t -> p (h t)"),
                                                                                start=(kb == 0), stop=(kb == nkb - 1)), reads=[vbuf, pm], writes=[OP])
                    S.op("pe", lambda pm=pm, kb=kb: nc.tensor.matmul(DEN.t[:], ones.t[:], pm.t[:].rearrange("p h t -> p (h t)"),
                                                                start=(kb == 0), stop=(kb == nkb - 1)), reads=[ones, pm], writes=[DEN])
            S.op("dve", lambda: nc.vector.reciprocal(rden.t[:], DEN.t[:]), reads=[DEN], writes=[rden])
            S.op("dve", lambda: nc.vector.tensor_tensor(onrm.t[:], OP.t[:].rearrange("p (h t) -> p h t", h=4), rden.t[:].rearrange("p (h t) -> p h t", h=4), op=ALU.mult),
                 reads=[OP, rden], writes=[onrm])
            S.op("dve", lambda n=n: nc.vector.tensor_tensor(ogt.t[:], onrm.t[:], sgj.t[:, 4 * n:4 * n + 4, :], op=ALU.mult), reads=[onrm, sgj], writes=[ogt])
            S.dma("sp", rOGT, OGT[4 * n:4 * n + 4, :, j * 128:(j + 1) * 128].rearrange("h d t -> d h t"), ogt.t[:], reads=[ogt])
    S.finish()
    return nc


def _run(nc, in_maps):
    res = run_bass_kernel_spmd(nc, in_maps, core_ids=list(range(NCORES)))
    return res.results


def host_gather_kv(outs, kt_key, v_key):
    KTA = np.zeros((4, 128, SEQ), ml_dtypes.bfloat16)
    VH = np.zeros((4, 128, 64, 128), ml_dtypes.bfloat16)
    for c in range(NCORES):
        kt = np.asarray(outs[c][kt_key])
        v = np.asarray(outs[c][v_key]).reshape(NBL, 128, 4, 128)
        for j in range(NBL):
            b = blk(c, j)
            KTA[:, :, b * 128:(b + 1) * 128] = kt[:, :, j * 128:(j + 1) * 128]
            VH[:, :, b, :] = v[j].transpose(1, 0, 2)
    return KTA, VH


def host_perj(a, nch):
    return np.ascontiguousarray(np.asarray(a).reshape(nch, 128, NBL, 128).transpose(2, 1, 0, 3))


def host_pen(c):
    p = np.arange(128)[:, None, None]
    j = np.arange(NBL)[None, :, None]
    i = np.arange(1024)[None, None, :]
    b = np.array([blk(c, jj) for jj in range(NBL)])[None, :, None]
    return np.where(1024 * j + i <= 128 * b + p, 0.0, NEG).astype(np.float32)


def stage_A(inputs, cs):
    x = np.asarray(inputs["x"])[0]
    pos = np.asarray(inputs["positions"]).astype(np.int32)
    wA = host_wA(np.asarray(inputs["w_in_a"]))
    ims = []
    for c in range(NCORES):
        tok = host_tokens(c)
        ims.append({"xT": np.ascontiguousarray(x[tok].T), "pos": np.ascontiguousarray(pos[:, tok]), "w": wA,
                    "ropec": cs["ropec"], "perm_qk": cs["perm_qk"], "perm_i": cs["perm_i"]})
    return _run(build_A(), ims)


def stage_B(oa, cs):
    KTA, VH = host_gather_kv(oa, "KT", "V")
    IKA = np.zeros((128, SEQ), ml_dtypes.bfloat16)
    for c in range(NCORES):
        ik = np.asarray(oa[c]["IKT"])
        for j in range(NBL):
            b = blk(c, j)
            IKA[:, b * 128:(b + 1) * 128] = ik[:, j * 128:(j + 1) * 128]
    ims = []
    for c in range(NCORES):
        ims.append({"QTJ": host_perj(oa[c]["QT"], 16), "SGJ": host_perj(oa[c]["SG"], 16), "IQJ": host_perj(oa[c]["IQT"], 8),
                    "IW": np.asarray(oa[c]["IW"]), "IKA": IKA, "KTA": KTA, "VH": VH, "PEN": host_pen(c),
                    "ident": cs["ident"], "ones_bf": cs["ones_bf"]})
    return _run(build_B(), ims)


def build_P(layer, with_proj):
    nc = bass.Bass("TRN2", target_bir_lowering=False)
    OGTd = nc.dram_tensor("OGT", [16, 128, TOK], BF16, kind="ExternalInput").ap()
    xT = nc.dram_tensor("xT", [DM, TOK], F32, kind="ExternalInput").ap()
    wo = nc.dram_tensor("wo", [DM, DM], F32, kind="ExternalInput").ap()
    lng_d = nc.dram_tensor("lng", [128, 16], F32, kind="ExternalInput").ap()
    lnb_d = nc.dram_tensor("lnb", [128, 16], F32, kind="ExternalInput").ap()
    pT = nc.dram_tensor("pT", [256, TOK], F32, kind="ExternalInput").ap()
    wple = nc.dram_tensor("wple", [256, DM], F32, kind="ExternalInput").ap()
    wg = nc.dram_tensor("wg", [DM, DM], F32, kind="ExternalInput").ap()
    onesf_d = nc.dram_tensor("ones_f", [128, 128], F32, kind="ExternalInput").ap()
    XO = nc.dram_tensor("XO", [DM, TOK], F32, kind="ExternalOutput").ap()
    if with_proj:
        wB = nc.dram_tensor("wB", [DM, 5120], F32, kind="ExternalInput").ap()
        QT = nc.dram_tensor("QT", [16, 128, TOK], BF16, kind="ExternalOutput").ap()
        KT = nc.dram_tensor("KT", [4, 128, TOK], BF16, kind="ExternalOutput").ap()
        SG = nc.dram_tensor("SG", [16, 128, TOK], BF16, kind="ExternalOutput").ap()
        V = nc.dram_tensor("V", [TOK, 512], BF16, kind="ExternalOutput").ap()
    S = Sched(nc)
    ogt = S.sbuf("ogt", [128, 16, TOK], BF16)
    yT = S.sbuf("yT", [128, 16, TOK], F32)
    wbufs = [S.sbuf("wb%d" % i, [128, 16, 512], BF16) for i in range(2)]
    ptb = S.sbuf("ptb", [128, 2, TOK], BF16)
    wpl = S.sbuf("wpl", [128, 2, DM], BF16)
    onesf = S.sbuf("onesf", [128, 128], F32)
    lng = S.sbuf("lng", [128, 16], F32)
    lnb = S.sbuf("lnb", [128, 16], F32)
    xin = [S.sbuf("xin%d" % i, [128, 512], F32) for i in range(2)]
    sq = [S.sbuf("sq%d" % i, [128, 512], F32) for i in range(2)]
    meanb = S.sbuf("meanb", [128, 512], F32)
    rstdb = S.sbuf("rstdb", [128, 512], F32)
    tmpf = [S.sbuf("tmpf%d" % i, [128, 512], F32) for i in range(2)]
    stg = [S.sbuf("stg%d" % i, [128, 512], BF16) for i in range(4)]
    psb = [S.psum("ps%d" % i, [128, 512], F32) for i in range(4)]
    SUM = S.psum("SUM", [128, 512], F32)
    SSQ = S.psum("SSQ", [128, 512], F32)
    rXO = S.dram(XO, "XO")
    S.dma("sp", ogt, ogt.t[:], OGTd.rearrange("h d t -> d h t"))
    S.dma("sp", onesf, onesf.t[:], onesf_d)
    S.dma("sp", lng, lng.t[:], lng_d)
    S.dma("sp", lnb, lnb.t[:], lnb_d)
    S.dma("pool", ptb, ptb.t[:], pT.rearrange("(c p) t -> p c t", p=128))
    S.dma("pool", wpl, wpl.t[:], wple.rearrange("(c p) n -> p c n", p=128))
    t1, t2 = tmpf
    cnt = 0
    for g in range(4):
        wb = _rr(wbufs, g)
        S.dma("pool", wb, wb.t[:], wo[:, 512 * g:512 * g + 512].rearrange("(c p) n -> p c n", p=128))
        for mi in range(4):
            m = 4 * g + mi
            for hh in range(2):
                sl = slice(hh * 512, (hh + 1) * 512)
                ps = _rr(psb, cnt)
                xi = _rr(xin, cnt)
                cnt += 1

                def mm(ps=ps, wb=wb, mi=mi, sl=sl):
                    last = None
                    for k in range(16):
                        last = nc.tensor.matmul(ps.t[:], wb.t[:, k, mi * 128:(mi + 1) * 128], ogt.t[:, k, sl], start=(k == 0), stop=(k == 15))
                    return last
                S.op("pe", mm, reads=[wb, ogt], writes=[ps])
                S.dma("sp", xi, xi.t[:], xT[m * 128:(m + 1) * 128, sl])
                S.op("dve", lambda ps=ps, xi=xi, m=m, sl=sl: nc.vector.scalar_tensor_tensor(yT.t[:, m, sl], xi.t[:], ALPHA, ps.t[:], op0=ALU.mult, op1=ALU.add),
                     reads=[xi, ps], writes=[yT])
    for hh in range(2):
        sl = slice(hh * 512, (hh + 1) * 512)
        for m in range(16):
            s_ = _rr(sq, m)
            S.op("pe", lambda m=m, sl=sl: nc.tensor.matmul(SUM.t[:], onesf.t[:], yT.t[:, m, sl], start=(m == 0), stop=(m == 15)), reads=[onesf, yT], writes=[SUM])
            S.op("act", lambda m=m, sl=sl, s_=s_: nc.scalar.activation(s_.t[:], yT.t[:, m, sl], AF.Square), reads=[yT], writes=[s_])
            S.op("pe", lambda m=m, s_=s_: nc.tensor.matmul(SSQ.t[:], onesf.t[:], s_.t[:], start=(m == 0), stop=(m == 15)), reads=[onesf, s_], writes=[SSQ])
        S.op("act", lambda: nc.scalar.activation(meanb.t[:], SUM.t[:], AF.Copy, scale=1.0 / DM), reads=[SUM], writes=[meanb])
        S.op("act", lambda: nc.scalar.activation(t1.t[:], SSQ.t[:], AF.Copy, scale=1.0 / DM), reads=[SSQ], writes=[t1])
        S.op("dve", lambda: nc.vector.tensor_tensor(t2.t[:], meanb.t[:], meanb.t[:], op=ALU.mult), reads=[meanb], writes=[t2])
        S.op("dve", lambda: nc.vector.tensor_tensor(t1.t[:], t1.t[:], t2.t[:], op=ALU.subtract), reads=[t1, t2], writes=[t1])
        S.op("dve", lambda: nc.vector.tensor_scalar(t1.t[:], t1.t[:], LN_EPS, None, op0=ALU.add), reads=[t1], writes=[t1])
        S.op("act", lambda: nc.scalar.activation(t2.t[:], t1.t[:], AF.Sqrt), reads=[t1], writes=[t2])
        S.op("dve", lambda: nc.vector.reciprocal(rstdb.t[:], t2.t[:]), reads=[t2], writes=[rstdb])
        for m in range(16):
            S.op("dve", lambda m=m, sl=sl: nc.vector.tensor_tensor(t1.t[:], yT.t[:, m, sl], meanb.t[:], op=ALU.subtract), reads=[yT, meanb], writes=[t1])
            S.op("dve", lambda: nc.vector.tensor_tensor(t2.t[:], t1.t[:], rstdb.t[:], op=ALU.mult), reads=[t1, rstdb], writes=[t2])
            S.op("dve", lambda m=m, sl=sl: nc.vector.tensor_scalar(yT.t[:, m, sl], t2.t[:], lng.t[:, m:m + 1], lnb.t[:, m:m + 1], op0=ALU.mult, op1=ALU.add),
                 reads=[t2, lng, lnb], writes=[yT])
            S.op("act", lambda m=m, sl=sl: nc.scalar.activation(ogt.t[:, m, sl], yT.t[:, m, sl], AF.Copy), reads=[yT], writes=[ogt])
    xlb = ogt
    for g in range(4):
        wb = _rr(wbufs, g)
        S.dma("pool", wb, wb.t[:], wg[:, 512 * g:512 * g + 512].rearrange("(c p) n -> p c n", p=128))
        for mi in range(4):
            m = 4 * g + mi
            for hh in range(2):
                sl = slice(hh * 512, (hh + 1) * 512)
                ps = _rr(psb, cnt)
                ps2 = _rr(psb, cnt + 1)
                cnt += 2

                def mm(ps=ps, wb=wb, mi=mi, sl=sl):
                    last = None
                    for k in range(16):
                        last = nc.tensor.matmul(ps.t[:], wb.t[:, k, mi * 128:(mi + 1) * 128], xlb.t[:, k, sl], start=(k == 0), stop=(k == 15))
                    return last
                S.op("pe", mm, reads=[wb, xlb], writes=[ps])
                S.op("act", lambda ps=ps: nc.scalar.activation(t1.t[:], ps.t[:], AF.Sigmoid), reads=[ps], writes=[t1])

                def mm2(ps2=ps2, m=m, sl=sl):
                    last = None
                    for k in range(2):
                        last = nc.tensor.matmul(ps2.t[:], wpl.t[:, k, m * 128:(m + 1) * 128], ptb.t[:, k, sl], start=(k == 0), stop=(k == 1))
                    return last
                S.op("pe", mm2, reads=[wpl, ptb], writes=[ps2])
                S.op("dve", lambda ps2=ps2: nc.vector.tensor_tensor(t2.t[:], ps2.t[:], t1.t[:], op=ALU.mult), reads=[ps2, t1], writes=[t2])
                S.op("dve", lambda m=m, sl=sl: nc.vector.tensor_tensor(yT.t[:, m, sl], yT.t[:, m, sl], t2.t[:], op=ALU.add), reads=[yT, t2], writes=[yT])
    S.dma("sp", rXO, XO.rearrange("(m p) t -> p m t", p=128), yT.t[:], reads=[yT])
    if with_proj:
        for m in range(16):
            S.op("act", lambda m=m: nc.scalar.activation(ogt.t[:, m, :], yT.t[:, m, :], AF.Copy), reads=[yT], writes=[ogt])
        rq, rk, rs, rv = [S.dram(a, n) for a, n in ((QT, "QT"), (KT, "KT"), (SG, "SG"), (V, "V"))]
        chunks = [("q1", QT[h], rq) for h in range(16)] + [("k1", KT[n], rk) for n in range(4)] + [("g", SG[h], rs) for h in range(16)]
        emit_proj(S, nc, ogt, wB, chunks, (36 * 128, V, rv, BF16), None, None, wbufs, stg, tmpf, psb)
    S.finish()
    return nc


def build_C():
    nc = bass.Bass("TRN2", target_bir_lowering=False)
    QTJ = nc.dram_tensor("QTJ", [NBL, 128, 16, 128], BF16, kind="ExternalInput").ap()
    SGJ = nc.dram_tensor("SGJ", [NBL, 128, 16, 128], BF16, kind="ExternalInput").ap()
    KTA = nc.dram_tensor("KTA", [4, 128, SEQ], BF16, kind="ExternalInput").ap()
    VH = nc.dram_tensor("VH", [4, 128, 64, 128], BF16, kind="ExternalInput").ap()
    CM = nc.dram_tensor("CM", [128, NBL, 8, 128], BF16, kind="ExternalInput").ap()
    utri_d = nc.dram_tensor("utri", [128, 128], BF16, kind="ExternalInput").ap()
    ones_d = nc.dram_tensor("ones_bf", [128, 128], BF16, kind="ExternalInput").ap()
    OGT = nc.dram_tensor("OGT", [16, 128, TOK], BF16, kind="ExternalOutput").ap()
    S = Sched(nc)
    utri = S.sbuf("utri", [128, 128], BF16)
    ones = S.sbuf("ones", [128, 128], BF16)
    S.dma("sp", utri, utri.t[:], utri_d)
    S.dma("sp", ones, ones.t[:], ones_d)
    qj = S.sbuf("qj", [128, 16, 128], BF16)
    sgj = S.sbuf("sgj", [128, 16, 128], BF16)
    cmj = S.sbuf("cmj", [128, 8, 128], BF16)
    es = [S.sbuf("e%d" % i, [128, 4, 128], F32) for i in range(3)]
    Ls = [S.sbuf("L%d" % i, [128, 4, 128], BF16) for i in range(3)]
    gxs = [S.sbuf("gx%d" % i, [128, 4, 128], F32) for i in range(3)]
    As = [S.sbuf("A%d" % i, [128, 4, 128], BF16) for i in range(3)]
    Lsum = S.sbuf("Lsum", [128, 4, 128], BF16)
    kbufs = [S.sbuf("kbuf%d" % i, [128, 1024], BF16) for i in range(3)]
    vbufs = [S.sbuf("vbuf%d" % i, [128, 8, 128], BF16) for i in range(3)]
    ogt = S.sbuf("ogt", [128, 4, 128], BF16)
    Z = [S.psum("Z%d" % i, [128, 512], F32) for i in range(3)]
    E = [S.psum("E%d" % i, [128, 512], F32) for i in range(3)]
    OP = S.psum("OP", [128, 512], F32)
    rOGT = S.dram(OGT, "OGT")
    wc = 0
    kvc = 0
    for j in range(NBL):
        nkb = 8 * (j + 1)
        S.dma("sp", qj, qj.t[:], QTJ[j])
        S.dma("sp", sgj, sgj.t[:], SGJ[j])
        S.dma("sp", cmj, cmj.t[:], CM[:, j, :, :])
        for n in range(4):
            for g in range(j, -1, -1):
                kbuf = _rr(kbufs, kvc)
                vbuf = _rr(vbufs, kvc)
                kvc += 1
                S.dma("sp", kbuf, kbuf.t[:], KTA[n, :, g * 1024:(g + 1) * 1024])
                S.dma("sp", vbuf, vbuf.t[:], VH[n, :, 8 * g:8 * g + 8, :])
                for r in range(7, -1, -1):
                    kb = 8 * g + r
                    first = (kb == nkb - 1)
                    diag = (g == j)
                    z = _rr(Z, wc)
                    Eb = _rr(E, wc)
                    e = _rr(es, wc)
                    L = _rr(Ls, wc)
                    gx = _rr(gxs, wc)
                    A = _rr(As, wc)
                    wc += 1
                    cmb = cmj.t[:, r, :].unsqueeze(1).to_broadcast([128, 4, 128])
                    S.op("pe", lambda z=z, kbuf=kbuf, r=r, n=n: nc.tensor.matmul(z.t[:], kbuf.t[:, r * 128:(r + 1) * 128], qj.t[:, 4 * n:4 * n + 4, :],
                                                                             start=True, stop=True), reads=[kbuf, qj], writes=[z])
                    S.op("act", lambda z=z, e=e: nc.scalar.activation(e.t[:], z.t[:].rearrange("p (h t) -> p h t", h=4), AF.Exp), reads=[z], writes=[e])
                    S.op("act", lambda e=e, L=L: nc.scalar.activation(L.t[:], e.t[:], AF.Ln, bias=1.0), reads=[e], writes=[L])
                    if diag:
                        S.op("dve", lambda L=L, cmb=cmb: nc.vector.tensor_tensor(L.t[:], L.t[:], cmb, op=ALU.mult), reads=[L, cmj], writes=[L])
                    S.op("pe", lambda Eb=Eb, L=L, first=first: nc.tensor.matmul(Eb.t[:], utri.t[:], L.t[:].rearrange("p h t -> p (h t)"), start=True, stop=first),
                         reads=[utri, L], writes=[Eb])
                    if not first:
                        S.op("pe", lambda Eb=Eb: nc.tensor.matmul(Eb.t[:], ones.t[:], Lsum.t[:].rearrange("p h t -> p (h t)"), start=False, stop=True),
                             reads=[ones, Lsum], writes=[Eb])
                    S.op("act", lambda Eb=Eb, gx=gx: nc.scalar.activation(gx.t[:], Eb.t[:].rearrange("p (h t) -> p h t", h=4), AF.Exp, scale=-1.0),
                         reads=[Eb], writes=[gx])
                    S.op("dve", lambda e=e, gx=gx, A=A: nc.vector.tensor_tensor(A.t[:], e.t[:], gx.t[:], op=ALU.mult), reads=[e, gx], writes=[A])
                    if diag:
                        S.op("dve", lambda A=A, cmb=cmb: nc.vector.tensor_tensor(A.t[:], A.t[:], cmb, op=ALU.mult), reads=[A, cmj], writes=[A])
                    S.op("pe", lambda A=A, vbuf=vbuf, r=r, first=first, kb=kb: nc.tensor.matmul(OP.t[:], vbuf.t[:, r, :], A.t[:].rearrange("p h t -> p (h t)"),
                                                                                          start=first, stop=(kb == 0)), reads=[vbuf, A], writes=[OP])
                    if first:
                        S.op("pool", lambda L=L: nc.gpsimd.tensor_copy(Lsum.t[:], L.t[:]), reads=[L], writes=[Lsum])
                    else:
                        S.op("pool", lambda L=L: nc.gpsimd.tensor_tensor(Lsum.t[:], Lsum.t[:], L.t[:], op=ALU.add), reads=[Lsum, L], writes=[Lsum])
            S.op("dve", lambda n=n: nc.vector.tensor_tensor(ogt.t[:], OP.t[:].rearrange("p (h t) -> p h t", h=4), sgj.t[:, 4 * n:4 * n + 4, :], op=ALU.mult),
                 reads=[OP, sgj], writes=[ogt])
            S.dma("sp", rOGT, OGT[4 * n:4 * n + 4, :, j * 128:(j + 1) * 128].rearrange("h d t -> d h t"), ogt.t[:], reads=[ogt])
    S.finish()
    return nc


def host_cm(c):
    s = np.arange(128)[:, None, None, None]
    j = np.arange(NBL)[None, :, None, None]
    r = np.arange(8)[None, None, :, None]
    t = np.arange(128)[None, None, None, :]
    b = np.array([blk(c, jj) for jj in range(NBL)])[None, :, None, None]
    return _bf((1024 * j + 128 * r + s < 128 * b + t).astype(np.float32))


def stage_P(layer, og, xTs, inputs, cs, with_proj):
    wo = np.asarray(inputs["w_o_a" if layer == 0 else "w_o_b"])[0]
    lng = np.ascontiguousarray(np.asarray(inputs["ln_g"])[layer].reshape(16, 128).T)
    lnb = np.ascontiguousarray(np.asarray(inputs["ln_b"])[layer].reshape(16, 128).T)
    p = np.asarray(inputs["p"])[layer, 0]
    wple = np.asarray(inputs["w_ple"])[layer]
    wg = np.asarray(inputs["w_ple_gate"])[layer]
    ims = []
    if with_proj:
        wib = np.asarray(inputs["w_in_b"])[0]
        wkv = np.asarray(inputs["w_kv_b"])
        wB = np.ascontiguousarray(np.concatenate([wib[:, :2048], wkv[:, :512], wib[:, 2048:], wkv[:, 512:]], axis=1))
    for c in range(NCORES):
        tok = host_tokens(c)
        im = {"OGT": np.asarray(og[c]["OGT"]), "xT": xTs[c], "wo": wo, "lng": lng, "lnb": lnb,
              "pT": np.ascontiguousarray(p[tok].T), "wple": wple, "wg": wg, "ones_f": cs["ones_f"]}
        if with_proj:
            im["wB"] = wB
        ims.append(im)
    return _run(build_P(layer, with_proj), ims)


def stage_C(op0, cs):
    KTA, VH = host_gather_kv(op0, "KT", "V")
    ims = []
    for c in range(NCORES):
        ims.append({"QTJ": host_perj(op0[c]["QT"], 16), "SGJ": host_perj(op0[c]["SG"], 16), "KTA": KTA, "VH": VH,
                    "CM": host_cm(c), "utri": cs["utri"], "ones_bf": cs["ones_bf"]})
    return _run(build_C(), ims)


def kernel(**inputs):
    cs = host_consts()
    x = np.asarray(inputs["x"])[0]
    xTs = [np.ascontiguousarray(x[host_tokens(c)].T) for c in range(NCORES)]
    oa = stage_A(inputs, cs)
    ob = stage_B(oa, cs)
    op0 = stage_P(0, ob, xTs, inputs, cs, True)
    x1Ts = [np.ascontiguousarray(np.asarray(op0[c]["XO"])) for c in range(NCORES)]
    oc = stage_C(op0, cs)
    op1 = stage_P(1, oc, x1Ts, inputs, cs, False)
    out = np.zeros((1, SEQ, DM), np.float32)
    for c in range(NCORES):
        out[0, host_tokens(c), :] = np.asarray(op1[c]["XO"]).T
    return out
```

```python
import numpy as np
from contextlib import ExitStack
import concourse.bass as bass
import concourse.mybir as mybir
from concourse.bass_utils import run_bass_kernel_spmd
import ml_dtypes

F32 = mybir.dt.float32
BF16 = mybir.dt.bfloat16
I32 = mybir.dt.int32
AF = mybir.ActivationFunctionType
ALU = mybir.AluOpType
AX = mybir.AxisListType


class _Ctr:
    def __init__(self, sem, name):
        self.sem = sem
        self.count = 0
        self.name = name


class _Res:
    def __init__(self, name, t=None):
        self.name = name
        self.t = t
        self.lw = None
        self.rd = {}
        self.dsem = None


class _Eng:
    def __init__(self, name, h, ctr):
        self.name = name
        self.h = h
        self.ctr = ctr
        self.seen = {}


class Sched:
    def __init__(self, nc):
        self.nc = nc
        self.es = ExitStack()
        self.eng = {}
        self.cc_sems = [self.es.enter_context(nc.semaphore("cc%d" % i)) for i in range(6)]
        for name, h in (("pe", nc.tensor), ("act", nc.scalar), ("dve", nc.vector),
                        ("pool", nc.gpsimd), ("sp", nc.sync)):
            sem = self.es.enter_context(nc.semaphore("sem_" + name))
            self.eng[name] = _Eng(name, h, _Ctr(sem, name))
        self.dma_res = []
        self.pes = None
        self.prefix = ""
        self.dreg = {}

    def begin_phase(self, prefix):
        self.pes = ExitStack()
        self.prefix = prefix

    def end_phase(self):
        self.barrier()
        self.pes.close()
        self.pes = None

    def barrier(self):
        ctrs = [e.ctr for e in self.eng.values()] + [r.dsem for r in self.dma_res]
        for E in self.eng.values():
            for c in ctrs:
                if c is E.ctr or c.count == 0:
                    continue
                if E.seen.get(c, 0) < c.count:
                    E.h.wait_ge(c.sem, c.count)
                    E.seen[c] = c.count

    def sbuf(self, name, shape, dtype):
        st = self.pes if self.pes is not None else self.es
        t = st.enter_context(self.nc.sbuf_tensor("sb_" + self.prefix + name, shape, dtype))
        return _Res(self.prefix + name, t)

    def psum(self, name, shape, dtype):
        st = self.pes if self.pes is not None else self.es
        t = st.enter_context(self.nc.psum_tensor("pp_" + self.prefix + name, shape, dtype))
        return _Res(self.prefix + name, t)

    def dram(self, ap, name):
        if name not in self.dreg:
            self.dreg[name] = _Res(name, ap)
        return self.dreg[name]

    def view(self, name):
        return _Res(name, None)

    def _dsem(self, r):
        if r.dsem is None:
            sem = self.es.enter_context(self.nc.semaphore("dsem_" + r.name))
            r.dsem = _Ctr(sem, r.name)
            self.dma_res.append(r)
        return r.dsem

    def _waits(self, E, reads, writes):
        need = {}

        def req(kv, raw):
            if kv is None:
                return
            key, val = kv
            if key is E.ctr:
                if not raw or E.name in ("pe", "sp"):
                    return
            if need.get(key, 0) < val:
                need[key] = val

        for r in reads:
            req(r.lw, True)
        for w in writes:
            req(w.lw, False)
            for k, v in w.rd.items():
                req((k, v), False)
        for key, val in need.items():
            if E.seen.get(key, 0) >= val:
                continue
            E.h.wait_ge(key.sem, val)
            E.seen[key] = val

    def op(self, eng, fn, reads=(), writes=()):
        E = self.eng[eng]
        self._waits(E, reads, writes)
        inst = fn()
        E.ctr.count += 1
        inst.then_inc(E.ctr.sem, 1)
        for w in writes:
            w.lw = (E.ctr, E.ctr.count)
            w.rd = {}
        for r in reads:
            if r not in writes:
                r.rd[E.ctr] = E.ctr.count
        return inst

    def dma(self, queue, dst, out_ap, in_ap, reads=(), cast=False):
        E = self.eng[queue]
        self._waits(E, reads, [dst])
        c = self._dsem(dst)
        inst = E.h.dma_start(out=out_ap, in_=in_ap)
        inst.then_inc(c.sem, 16)
        c.count += 16
        dst.lw = (c, c.count)
        dst.rd = {}
        for r in reads:
            r.rd[c] = c.count
        return inst

    def collective(self, kind, dst, out_ap, in_ap, reads=()):
        E = self.eng["pool"]
        self._waits(E, reads, [dst])
        if dst.dsem is None:
            dst.dsem = _Ctr(self.cc_sems.pop(0), dst.name)
            self.dma_res.append(dst)
        c = dst.dsem
        inst = E.h.collective_compute(kind, ALU.bypass, replica_groups=[list(range(NCORES))], ins=[in_ap], outs=[out_ap])
        inst.then_inc(c.sem, 1)
        c.count += 1
        dst.lw = (c, c.count)
        dst.rd = {}
        for r in reads:
            r.rd[c] = c.count
        return inst

    def finish(self):
        sp = self.eng["sp"]
        for r in self.dma_res:
            if r.dsem.count > 0 and sp.seen.get(r.dsem, 0) < r.dsem.count:
                sp.h.wait_ge(r.dsem.sem, r.dsem.count)
        for name in ("pe", "act", "dve", "pool"):
            c = self.eng[name].ctr
            if c.count > 0:
                sp.h.wait_ge(c.sem, c.count)
        self.es.close()


NCORES = 8
NBL = 8
TOK = 1024
DM = 2048
SEQ = 8192
THETA = 500000.0
ALPHA = 4.0 ** 0.25
LN_EPS = 1e-5
NEG = -1.0e30
PI = float(np.pi)
C1 = 6.28125
C2 = float(2.0 * np.pi - 6.28125)
NBIS = 24


def blk(c, j):
    return 16 * (j // 2) + (c if j % 2 == 0 else 15 - c)


def _rr(lst, i):
    return lst[i % len(lst)]


def emit_rope_tables(S, nc, pos_ap, ropec, ci, cosT, sinT, tmp):
    posi, posf, ang, kf, ki, r, m = tmp
    S.dma("sp", posi, posi.t[:], pos_ap.partition_broadcast(128))
    S.op("dve", lambda: nc.vector.tensor_copy(posf.t[:], posi.t[:]), reads=[posi], writes=[posf])
    for which, dst in ((0, sinT), (1, cosT)):
        if which == 0:
            S.op("dve", lambda: nc.vector.tensor_scalar(ang.t[:], posf.t[:], ropec.t[:, ci:ci + 1], None, op0=ALU.mult),
                 reads=[posf, ropec], writes=[ang])
        else:
            S.op("dve", lambda: nc.vector.tensor_scalar(ang.t[:], posf.t[:], ropec.t[:, ci:ci + 1], PI / 2, op0=ALU.mult, op1=ALU.add),
                 reads=[posf, ropec], writes=[ang])
        S.op("dve", lambda: nc.vector.tensor_scalar(ki.t[:], ang.t[:], 1.0 / (2 * PI), None, op0=ALU.mult), reads=[ang], writes=[ki])
        S.op("dve", lambda: nc.vector.tensor_copy(kf.t[:], ki.t[:]), reads=[ki], writes=[kf])
        S.op("dve", lambda: nc.vector.scalar_tensor_tensor(r.t[:], kf.t[:], -C1, ang.t[:], op0=ALU.mult, op1=ALU.add),
             reads=[kf, ang], writes=[r])
        S.op("dve", lambda: nc.vector.scalar_tensor_tensor(ang.t[:], kf.t[:], -C2, r.t[:], op0=ALU.mult, op1=ALU.add),
             reads=[kf, r], writes=[ang])
        S.op("dve", lambda: nc.vector.tensor_scalar(m.t[:], ang.t[:], PI, -2 * PI, op0=ALU.is_gt, op1=ALU.mult), reads=[ang], writes=[m])
        S.op("dve", lambda: nc.vector.tensor_tensor(r.t[:], ang.t[:], m.t[:], op=ALU.add), reads=[ang, m], writes=[r])
        S.op("dve", lambda: nc.vector.tensor_scalar(m.t[:], r.t[:], -PI, 2 * PI, op0=ALU.is_lt, op1=ALU.mult), reads=[r], writes=[m])
        S.op("dve", lambda: nc.vector.tensor_tensor(ang.t[:], r.t[:], m.t[:], op=ALU.add), reads=[r, m], writes=[ang])
        S.op("dve", lambda: nc.vector.tensor_scalar(r.t[:], ang.t[:], PI, -PI, op0=ALU.min, op1=ALU.max), reads=[ang], writes=[r])
        S.op("act", lambda: nc.scalar.activation(dst.t[:], r.t[:], AF.Sin), reads=[r], writes=[dst])
        if which == 0:
            S.op("dve", lambda: nc.vector.tensor_scalar(dst.t[:], dst.t[:], ropec.t[:, ci + 1:ci + 2], None, op0=ALU.mult),
                 reads=[dst, ropec], writes=[dst])


def emit_proj(S, nc, xb, w_ap, chunks, v_spec, iw_spec, rope, wbufs, stg, tmpf, psb):
    ngroups = (len(chunks) + 3) // 4
    cnt = 0
    for g in range(ngroups):
        wb = _rr(wbufs, g)
        gch = chunks[4 * g:4 * g + 4]
        ncol = 128 * len(gch)
        S.dma("pool", wb, wb.t[:, :, 0:ncol], w_ap[:, 512 * g:512 * g + ncol].rearrange("(c p) n -> p c n", p=128))
        for ci, (kind, oap, ores) in enumerate(gch):
            for hh in range(2):
                ps = _rr(psb, cnt)
                st = _rr(stg, cnt)
                cnt += 1

                def mm(ps=ps, wb=wb, ci=ci, hh=hh):
                    last = None
                    for k in range(16):
                        last = nc.tensor.matmul(ps.t[:], wb.t[:, k, ci * 128:(ci + 1) * 128], xb.t[:, k, hh * 512:(hh + 1) * 512],
                                                start=(k == 0), stop=(k == 15))
                    return last
                S.op("pe", mm, reads=[wb, xb], writes=[ps])
                sl = slice(hh * 512, (hh + 1) * 512)
                if kind == "g":
                    S.op("act", lambda ps=ps, st=st: nc.scalar.activation(st.t[:], ps.t[:], AF.Silu), reads=[ps], writes=[st])
                else:
                    sc = (128.0 ** -0.5) if kind in ("q", "q1") else 1.0
                    S.op("act", lambda ps=ps, st=st, sc=sc: nc.scalar.activation(st.t[:], ps.t[:], AF.Copy, scale=sc), reads=[ps], writes=[st])
                    if kind in ("q", "k", "iq"):
                        perm, cosT, sinT = rope["qk"] if kind in ("q", "k") else rope["i"]
                        ps2 = _rr(psb, cnt)
                        cnt += 1
                        t1, t2 = tmpf
                        S.op("pe", lambda ps2=ps2, st=st, perm=perm: nc.tensor.matmul(ps2.t[:], perm.t[:], st.t[:], start=True, stop=True),
                             reads=[perm, st], writes=[ps2])
                        S.op("dve", lambda st=st, cosT=cosT, sl=sl: nc.vector.tensor_tensor(t1.t[:], st.t[:], cosT.t[:, sl], op=ALU.mult),
                             reads=[st, cosT], writes=[t1])
                        S.op("dve", lambda ps2=ps2, sinT=sinT, sl=sl: nc.vector.tensor_tensor(t2.t[:], ps2.t[:], sinT.t[:, sl], op=ALU.mult),
                             reads=[ps2, sinT], writes=[t2])
                        S.op("dve", lambda st=st: nc.vector.tensor_tensor(st.t[:], t1.t[:], t2.t[:], op=ALU.add), reads=[t1, t2], writes=[st])
                S.dma("sp", ores, oap[:, sl], st.t[:], reads=[st])
    for spec, width in ((v_spec, 512), (iw_spec, 16)):
        if spec is None:
            continue
        col0, oap, ores, odt = spec
        wb = _rr(wbufs, ngroups + (0 if width == 512 else 1))
        S.dma("pool", wb, wb.t[:, :, 0:width], w_ap[:, col0:col0 + width].rearrange("(c p) n -> p c n", p=128))
        for tt in range(8):
            ps = _rr(psb, cnt)
            cnt += 1

            def mm(ps=ps, wb=wb, tt=tt, width=width):
                last = None
                for k in range(16):
                    last = nc.tensor.matmul(ps.t[:, 0:width], xb.t[:, k, tt * 128:(tt + 1) * 128], wb.t[:, k, 0:width],
                                            start=(k == 0), stop=(k == 15))
                return last
            S.op("pe", mm, reads=[wb, xb], writes=[ps])
            if width == 512:
                st = _rr(stg, cnt)
                S.op("act", lambda ps=ps, st=st: nc.scalar.activation(st.t[:], ps.t[:], AF.Copy), reads=[ps], writes=[st])
                S.dma("sp", ores, oap[tt * 128:(tt + 1) * 128, :], st.t[:], reads=[st])
            else:
                t1 = tmpf[0]
                S.op("act", lambda ps=ps, t1=t1: nc.scalar.activation(t1.t[:, 0:16], ps.t[:, 0:16], AF.Copy), reads=[ps], writes=[t1])
                S.dma("sp", ores, oap[tt * 128:(tt + 1) * 128, :], t1.t[:, 0:16], reads=[t1])


def emit_A(S, nc, D):
    S.begin_phase("A_")
    xb = S.sbuf("xb", [128, 16, TOK], BF16)
    S.dma("pool", xb, xb.t[:], D["xT"].rearrange("(c p) t -> p c t", p=128))
    ropec = S.sbuf("ropec", [128, 4], F32)
    pqk = S.sbuf("pqk", [128, 128], BF16)
    pii = S.sbuf("pii", [128, 128], BF16)
    S.dma("sp", ropec, ropec.t[:], D["ropec"])
    S.dma("sp", pqk, pqk.t[:], D["perm_qk"])
    S.dma("sp", pii, pii.t[:], D["perm_i"])
    tabs = [S.sbuf("tab%d" % i, [128, TOK], F32) for i in range(4)]
    posi = S.sbuf("posi", [128, TOK], I32)
    ki = S.sbuf("ki", [128, TOK], I32)
    tmp = [posi] + [S.sbuf("rt%d" % i, [128, TOK], F32) for i in range(3)] + [ki] + [S.sbuf("rt%d" % i, [128, TOK], F32) for i in range(3, 5)]
    emit_rope_tables(S, nc, D["pos"], ropec, 0, tabs[0], tabs[1], tmp)
    emit_rope_tables(S, nc, D["pos"], ropec, 2, tabs[2], tabs[3], tmp)
    rope = {"qk": (pqk, tabs[0], tabs[1]), "i": (pii, tabs[2], tabs[3])}
    wbufs = [S.sbuf("wb%d" % i, [128, 16, 512], BF16) for i in range(2)]
    stg = [S.sbuf("stg%d" % i, [128, 512], BF16) for i in range(4)]
    tmpf = [S.sbuf("tmpf%d" % i, [128, 512], F32) for i in range(2)]
    psb = [S.psum("ps%d" % i, [128, 512], F32) for i in range(4)]
    QT, KT, SG, IQT, IKT, V, IW = D["QT0"], D["KT0"], D["SG0"], D["IQT0"], D["IK0"], D["V0"], D["IW0"]
    rq, rk, rs, riq, rik, rv, riw = [S.dram(a, n) for a, n in ((QT, "QT0"), (KT, "KT0"), (SG, "SG0"), (IQT, "IQT0"), (IKT, "IK0"), (V, "V0"), (IW, "IW0"))]
    chunks = [("q", QT[h * 128:(h + 1) * 128, :], rq) for h in range(16)] + [("k", KT[n * 128:(n + 1) * 128, :], rk) for n in range(4)] + \
             [("g", SG[h * 128:(h + 1) * 128, :], rs) for h in range(16)] + [("iq", IQT[h * 128:(h + 1) * 128, :], riq) for h in range(8)] + [("iq", IKT, rik)]
    emit_proj(S, nc, xb, D["wA"], chunks, (45 * 128, V, rv, BF16), (45 * 128 + 512, IW, riw, F32), rope, wbufs, stg, tmpf, psb)
    S.end_phase()


def _pipeline(n_items, stages):
    maxlag = max(l for l, _ in stages)
    for step in range(n_items + maxlag):
        for lag, fn in stages:
            i = step - lag
            if 0 <= i < n_items:
                fn(i)


def _kv_views(KTG, VG):
    KTGv = KTG.rearrange("(c n d) t -> c n d t", c=NCORES, n=4, d=128)
    VGv = VG.rearrange("(c j p) (n d) -> c j p n d", c=NCORES, j=NBL, p=128, n=4)
    return KTGv, VGv


def emit_B(S, nc, D):
    S.begin_phase("B_")
    QT, SG, IQT, IWd, OGT = D["QT0"], D["SG0"], D["IQT0"], D["IW0"], D["OGT0"]
    rq, rs, riq, riw = S.dram(QT, "QT0"), S.dram(SG, "SG0"), S.dram(IQT, "IQT0"), S.dram(IWd, "IW0")
    rkg, rvg, rig = S.dram(D["KTG0"], "KTG0"), S.dram(D["VG0"], "VG0"), S.dram(D["IKG0"], "IKG0")
    KTGv, VGv = _kv_views(D["KTG0"], D["VG0"])
    IKGv = D["IKG0"].rearrange("(c p) t -> c p t", c=NCORES)
    QTv = QT.rearrange("(h d) t -> d h t", d=128)
    SGv = SG.rearrange("(h d) t -> d h t", d=128)
    IQv = IQT.rearrange("(h d) t -> d h t", d=128)
    OGTv = OGT.rearrange("(h d) t -> d h t", d=128)
    rOGT = S.dram(OGT, "OGT0")
    ident = S.sbuf("ident", [128, 128], BF16)
    ones = S.sbuf("ones", [128, 128], BF16)
    ikt = S.sbuf("ikt", [128, 64, 128], BF16)
    S.dma("sp", ident, ident.t[:], D["ident"])
    S.dma("sp", ones, ones.t[:], D["ones_bf"])
    for g in range(NBL):
        S.dma("sp", ikt, ikt.t[:, 8 * g:8 * g + 8, :], IKGv[:, :, g * 128:(g + 1) * 128].rearrange("c p t -> p c t"), reads=[rig])
    qj = S.sbuf("qj", [128, 16, 128], BF16)
    sgj = S.sbuf("sgj", [128, 16, 128], BF16)
    iqjs = [S.sbuf("iqj%d" % i, [128, 8, 128], BF16) for i in range(2)]
    iwjs = [S.sbuf("iwj%d" % i, [128, 16], F32) for i in range(2)]
    penjs = [S.sbuf("penj%d" % i, [128, 1024], F32) for i in range(2)]
    absws = [S.sbuf("absw%d" % i, [128, 16], F32) for i in range(2)]
    sgns = [S.sbuf("sgn%d" % i, [128, 16], F32) for i in range(2)]
    Dgs = [S.sbuf("Dg%d" % i, [128, 16, 128], BF16) for i in range(2)]
    iscs = [S.sbuf("isc%d" % i, [128, SEQ], F32) for i in range(2)]
    Mb = S.sbuf("Mb", [128, SEQ], BF16)
    MT = S.sbuf("MT", [128, SEQ], BF16)
    pw = S.sbuf("pw", [128, NBIS], F32)
    halfs = S.sbuf("halfs", [128, NBIS], F32)
    lo = S.sbuf("lo", [128, 1], F32)
    hi = S.sbuf("hi", [128, 1], F32)
    rng = S.sbuf("rng", [128, 1], F32)
    mid = S.sbuf("mid", [128, 1], F32)
    cntt = S.sbuf("cntt", [128, 1], F32)
    step = S.sbuf("step", [128, 1], F32)
    rbs = [S.sbuf("rb%d" % i, [128, 512], BF16) for i in range(4)]
    pts = [S.sbuf("pt%d" % i, [128, 4, 128], BF16) for i in range(4)]
    pms = [S.sbuf("pm%d" % i, [128, 4, 128], BF16) for i in range(4)]
    kbufs = [S.sbuf("kbuf%d" % i, [128, 8, 128], BF16) for i in range(3)]
    vbufs = [S.sbuf("vbuf%d" % i, [128, 8, 128], BF16) for i in range(3)]
    rden = S.sbuf("rden", [128, 512], F32)
    onrm = S.sbuf("onrm", [128, 4, 128], F32)
    ogt = S.sbuf("ogt", [128, 4, 128], BF16)
    W = [S.psum("W%d" % i, [128, 512], F32) for i in range(4)]
    ISC = S.psum("ISC", [128, 512], F32)
    TP = S.psum("TP", [128, 1024], BF16)
    OP = S.psum("OP", [128, 512], F32)
    DEN = S.psum("DEN", [128, 512], F32)
    for k in range(NBIS):
        S.op("dve", lambda k=k: nc.vector.memset(pw.t[:, k:k + 1], 2.0 ** -(k + 1)), writes=[pw])
    st_ = {"wc": 0, "kvc": 0}

    def prep_idx(j):
        tsl = slice(j * 128, (j + 1) * 128)
        iqj, iwj, penj, absw, sgn, Dg = iqjs[j % 2], iwjs[j % 2], penjs[j % 2], absws[j % 2], sgns[j % 2], Dgs[j % 2]
        S.dma("sp", iqj, iqj.t[:], IQv[:, :, tsl], reads=[riq])
        S.dma("sp", iwj, iwj.t[:], IWd[tsl, :], reads=[riw])
        S.dma("sp", penj, penj.t[:], D["PEN"][:, j, :])
        S.op("act", lambda: nc.scalar.activation(absw.t[:], iwj.t[:], AF.Abs), reads=[iwj], writes=[absw])
        S.op("dve", lambda: nc.vector.tensor_scalar(sgn.t[:], iwj.t[:], 0.0, 2.0, op0=ALU.is_ge, op1=ALU.mult), reads=[iwj], writes=[sgn])
        S.op("dve", lambda: nc.vector.tensor_scalar(sgn.t[:], sgn.t[:], -1.0, None, op0=ALU.add), reads=[sgn], writes=[sgn])
        for h in range(16):
            S.op("dve", lambda h=h: nc.vector.tensor_scalar(Dg.t[:, h, :], ident.t[:], sgn.t[:, h:h + 1], None, op0=ALU.mult),
                 reads=[ident, sgn], writes=[Dg])

    def idx(j):
        Sj = 1024 * (j + 1)
        iqj, absw, Dg, isc = iqjs[j % 2], absws[j % 2], Dgs[j % 2], iscs[j % 2]
        items = [(c, h) for c in range(Sj // 512) for h in range(16)]
        dbuf = {}

        def ix_s0(i):
            c, h = items[i]
            pb = 64 * (h % 2)
            d = _rr(W, st_["wc"])
            st_["wc"] += 1
            rb = _rr(rbs, i)
            dbuf[i] = rb
            S.op("pe", lambda: nc.tensor.matmul(d.t[:], iqj.t[pb:pb + 64, h // 2, :], ikt.t[pb:pb + 64, 4 * c:4 * c + 4, :],
                                                start=True, stop=True), reads=[iqj, ikt], writes=[d])
            S.op("act", lambda: nc.scalar.activation(rb.t[:], d.t[:], AF.Relu, scale=absw.t[:, h:h + 1]), reads=[d, absw], writes=[rb])

        def ix_s1(i):
            c, h = items[i]
            rb = dbuf.pop(i)
            S.op("pe", lambda: nc.tensor.matmul(ISC.t[:], Dg.t[:, h, :], rb.t[:], start=(h == 0), stop=(h == 15)), reads=[Dg, rb], writes=[ISC])
            if h == 15:
                S.op("act", lambda: nc.scalar.activation(isc.t[:, c * 512:(c + 1) * 512], ISC.t[:], AF.Copy), reads=[ISC], writes=[isc])

        _pipeline(len(items), [(0, ix_s0), (2, ix_s1)])

    def bis(j):
        Sj = 1024 * (j + 1)
        isc, penj = iscs[j % 2], penjs[j % 2]
        S.op("dve", lambda: nc.vector.tensor_reduce(lo.t[:], isc.t[:, 0:Sj], axis=AX.X, op=ALU.min), reads=[isc], writes=[lo])
        S.op("dve", lambda: nc.vector.tensor_tensor(isc.t[:, Sj - 1024:Sj], isc.t[:, Sj - 1024:Sj], penj.t[:], op=ALU.add), reads=[isc, penj], writes=[isc])
        S.op("dve", lambda: nc.vector.tensor_reduce(hi.t[:], isc.t[:, 0:Sj], axis=AX.X, op=ALU.max), reads=[isc], writes=[hi])
        S.op("dve", lambda: nc.vector.tensor_tensor(rng.t[:], hi.t[:], lo.t[:], op=ALU.subtract), reads=[hi, lo], writes=[rng])
        S.op("dve", lambda: nc.vector.tensor_scalar(halfs.t[:], pw.t[:], rng.t[:, 0:1], None, op0=ALU.mult), reads=[pw, rng], writes=[halfs])
        for k in range(NBIS):
            S.op("dve", lambda k=k: nc.vector.tensor_tensor(mid.t[:], lo.t[:], halfs.t[:, k:k + 1], op=ALU.add), reads=[lo, halfs], writes=[mid])
            S.op("dve", lambda: nc.vector.tensor_scalar(Mb.t[:, 0:Sj], isc.t[:, 0:Sj], mid.t[:, 0:1], None, op0=ALU.is_ge, op1=ALU.add, accum_out=cntt.t[:]),
                 reads=[isc, mid], writes=[Mb, cntt])
            S.op("dve", lambda k=k: nc.vector.scalar_tensor_tensor(step.t[:], cntt.t[:], 255.5, halfs.t[:, k:k + 1], op0=ALU.is_ge, op1=ALU.mult),
                 reads=[cntt, halfs], writes=[step])
            S.op("dve", lambda: nc.vector.tensor_tensor(lo.t[:], lo.t[:], step.t[:], op=ALU.add), reads=[lo, step], writes=[lo])
        S.op("dve", lambda: nc.vector.tensor_scalar(Mb.t[:, 0:Sj], isc.t[:, 0:Sj], lo.t[:, 0:1], None, op0=ALU.is_ge), reads=[isc, lo], writes=[Mb])
        for g in range(Sj // 1024):
            for r in range(8):
                kb = 8 * g + r
                S.op("pe", lambda r=r, kb=kb: nc.tensor.transpose(TP.t[:, r * 128:(r + 1) * 128], Mb.t[:, kb * 128:(kb + 1) * 128], ident.t[:]),
                     reads=[Mb, ident], writes=[TP])
            if g % 2 == 0:
                S.op("act", lambda g=g: nc.scalar.activation(MT.t[:, g * 1024:(g + 1) * 1024], TP.t[:], AF.Copy), reads=[TP], writes=[MT])
            else:
                S.op("dve", lambda g=g: nc.vector.tensor_copy(MT.t[:, g * 1024:(g + 1) * 1024], TP.t[:]), reads=[TP], writes=[MT])

    def att(j):
        Sj = 1024 * (j + 1)
        tsl = slice(j * 128, (j + 1) * 128)
        nkb = Sj // 128
        S.dma("sp", qj, qj.t[:], QTv[:, :, tsl], reads=[rq])
        S.dma("sp", sgj, sgj.t[:], SGv[:, :, tsl], reads=[rs])
        for n in range(4):
            tiles = [(g, r) for g in range(j + 1) for r in range(8)]
            kvb = {}
            tb = {}

            def load_piece(g, n=n):
                kbuf = _rr(kbufs, st_["kvc"])
                vbuf = _rr(vbufs, st_["kvc"])
                st_["kvc"] += 1
                kvb[g] = (kbuf, vbuf)
                S.dma("sp", kbuf, kbuf.t[:], KTGv[:, n, :, g * 128:(g + 1) * 128].rearrange("c d t -> d c t"), reads=[rkg])
                S.dma("sp", vbuf, vbuf.t[:], VGv[:, g, :, n, :].rearrange("c p d -> p c d"), reads=[rvg])

            def at_s0(i, n=n):
                g, r = tiles[i]
                if r == 0:
                    if g == 0:
                        load_piece(0)
                    if g + 1 <= j:
                        load_piece(g + 1)
                kbuf, vbuf = kvb[g]
                kb = 8 * g + r
                st = _rr(W, st_["wc"])
                st_["wc"] += 1
                pt = _rr(pts, i)
                pm = _rr(pms, i)
                tb[i] = (pm, vbuf)
                S.op("pe", lambda: nc.tensor.matmul(st.t[:], kbuf.t[:, r, :], qj.t[:, 4 * n:4 * n + 4, :], start=True, stop=True), reads=[kbuf, qj], writes=[st])
                S.op("act", lambda: nc.scalar.activation(pt.t[:], st.t[:].rearrange("p (h t) -> p h t", h=4), AF.Exp), reads=[st], writes=[pt])
                S.op("dve", lambda: nc.vector.tensor_tensor(
                    pm.t[:], pt.t[:], MT.t[:, kb * 128:(kb + 1) * 128].unsqueeze(1).to_broadcast([128, 4, 128]), op=ALU.mult),
                    reads=[pt, MT], writes=[pm])

            def at_s1(i):
                g, r = tiles[i]
                kb = 8 * g + r
                pm, vbuf = tb.pop(i)
                S.op("pe", lambda: nc.tensor.matmul(OP.t[:], vbuf.t[:, r, :], pm.t[:].rearrange("p h t -> p (h t)"),
                                                    start=(kb == 0), stop=(kb == nkb - 1)), reads=[vbuf, pm], writes=[OP])
                S.op("pe", lambda: nc.tensor.matmul(DEN.t[:], ones.t[:], pm.t[:].rearrange("p h t -> p (h t)"),
                                                    start=(kb == 0), stop=(kb == nkb - 1)), reads=[ones, pm], writes=[DEN])

            _pipeline(len(tiles), [(0, at_s0), (2, at_s1)])
            S.op("dve", lambda: nc.vector.reciprocal(rden.t[:], DEN.t[:]), reads=[DEN], writes=[rden])
            S.op("dve", lambda: nc.vector.tensor_tensor(onrm.t[:], OP.t[:].rearrange("p (h t) -> p h t", h=4), rden.t[:].rearrange("p (h t) -> p h t", h=4), op=ALU.mult),
                 reads=[OP, rden], writes=[onrm])
            S.op("dve", lambda n=n: nc.vector.tensor_tensor(ogt.t[:], onrm.t[:], sgj.t[:, 4 * n:4 * n + 4, :], op=ALU.mult), reads=[onrm, sgj], writes=[ogt])
            S.dma("sp", rOGT, OGTv[:, 4 * n:4 * n + 4, tsl], ogt.t[:], reads=[ogt])

    prep_idx(0)
    idx(0)
    for j in range(NBL):
        if j + 1 < NBL:
            prep_idx(j + 1)
            idx(j + 1)
        bis(j)
        att(j)
    S.end_phase()


def _bf(a):
    return np.asarray(a, dtype=np.float32).astype(ml_dtypes.bfloat16)


def host_consts():
    p = np.arange(128)
    ropec = np.zeros((128, 4), np.float32)
    inv_qk = (np.float32(1.0) / np.power(np.float32(THETA), np.arange(16, dtype=np.float32) / np.float32(16))).astype(np.float32)
    inv_i = (np.float32(1.0) / np.power(np.float32(THETA), np.arange(8, dtype=np.float32) / np.float32(8))).astype(np.float32)
    for q in range(128):
        if q < 32:
            ropec[q, 0] = inv_qk[q % 16]
            ropec[q, 1] = -1.0 if q < 16 else 1.0
        r = q % 64
        if r < 16:
            ropec[q, 2] = inv_i[r % 8]
            ropec[q, 3] = -1.0 if r < 8 else 1.0
    pqk = np.zeros((128, 128), np.float32)
    pii = np.zeros((128, 128), np.float32)
    for m in range(128):
        pm = m + 16 if m < 16 else (m - 16 if m < 32 else m)
        pqk[pm, m] = 1.0
        r = m % 64
        pm = m + 8 if r < 8 else (m - 8 if r < 16 else m)
        pii[pm, m] = 1.0
    ident = np.eye(128, dtype=np.float32)
    utri = (p[:, None] >= p[None, :]).astype(np.float32)
    return {"ropec": ropec, "perm_qk": _bf(pqk), "perm_i": _bf(pii), "ident": _bf(ident),
            "ones_bf": _bf(np.ones((128, 128))), "ones_f": np.ones((128, 128), np.float32), "utri": _bf(utri)}


def host_tokens(c):
    return np.concatenate([np.arange(128 * blk(c, j), 128 * blk(c, j) + 128) for j in range(NBL)])


def host_wA(w_in_a):
    w = w_in_a[0]
    q, k, v, gate, iq, ik, iw = np.split(w, [2048, 2560, 3072, 5120, 6144, 6208], axis=1)
    return np.ascontiguousarray(np.concatenate([q, k, gate, iq, ik, ik, v, iw], axis=1))


def emit_P(S, nc, D, layer, with_proj):
    L = str(layer)
    S.begin_phase("P%s_" % L)
    OGTd, xT, XO = D["OGT" + L], D["xin" + L], D["xout" + L]
    rog, rxin, rXO = S.dram(OGTd, "OGT" + L), S.dram(xT, "xin" + L), S.dram(XO, "xout" + L)
    wo, wg, wple, pT = D["wo" + L], D["wg" + L], D["wple" + L], D["pT" + L]
    ogt = S.sbuf("ogt", [128, 16, TOK], BF16)
    yT = S.sbuf("yT", [128, 16, TOK], F32)
    wbufs = [S.sbuf("wb%d" % i, [128, 16, 512], BF16) for i in range(2)]
    ptb = S.sbuf("ptb", [128, 2, TOK], BF16)
    wpl = S.sbuf("wpl", [128, 2, DM], BF16)
    onesf = S.sbuf("onesf", [128, 128], F32)
    lng = S.sbuf("lng", [128, 16], F32)
    lnb = S.sbuf("lnb", [128, 16], F32)
    xin = [S.sbuf("xin%d" % i, [128, 512], F32) for i in range(2)]
    sq = [S.sbuf("sq%d" % i, [128, 512], F32) for i in range(2)]
    meanb = S.sbuf("meanb", [128, 512], F32)
    rstdb = S.sbuf("rstdb", [128, 512], F32)
    tmpf = [S.sbuf("tmpf%d" % i, [128, 512], F32) for i in range(2)]
    stg = [S.sbuf("stg%d" % i, [128, 512], BF16) for i in range(4)]
    psb = [S.psum("ps%d" % i, [128, 512], F32) for i in range(4)]
    SUM = S.psum("SUM", [128, 512], F32)
    SSQ = S.psum("SSQ", [128, 512], F32)
    S.dma("sp", ogt, ogt.t[:], OGTd.rearrange("(h d) t -> d h t", d=128), reads=[rog])
    S.dma("sp", onesf, onesf.t[:], D["ones_f"])
    S.dma("sp", lng, lng.t[:], D["lng" + L])
    S.dma("sp", lnb, lnb.t[:], D["lnb" + L])
    S.dma("pool", ptb, ptb.t[:], pT.rearrange("(c p) t -> p c t", p=128))
    S.dma("pool", wpl, wpl.t[:], wple.rearrange("(c p) n -> p c n", p=128))
    t1, t2 = tmpf
    cnt = 0
    for g in range(4):
        wb = _rr(wbufs, g)
        S.dma("pool", wb, wb.t[:], wo[:, 512 * g:512 * g + 512].rearrange("(c p) n -> p c n", p=128))
        for mi in range(4):
            m = 4 * g + mi
            for hh in range(2):
                sl = slice(hh * 512, (hh + 1) * 512)
                ps = _rr(psb, cnt)
                xi = _rr(xin, cnt)
                cnt += 1

                def mm(ps=ps, wb=wb, mi=mi, sl=sl):
                    last = None
                    for k in range(16):
                        last = nc.tensor.matmul(ps.t[:], wb.t[:, k, mi * 128:(mi + 1) * 128], ogt.t[:, k, sl], start=(k == 0), stop=(k == 15))
                    return last
                S.op("pe", mm, reads=[wb, ogt], writes=[ps])
                S.dma("sp", xi, xi.t[:], xT[m * 128:(m + 1) * 128, sl], reads=[rxin])
                S.op("dve", lambda ps=ps, xi=xi, m=m, sl=sl: nc.vector.scalar_tensor_tensor(yT.t[:, m, sl], xi.t[:], ALPHA, ps.t[:], op0=ALU.mult, op1=ALU.add),
                     reads=[xi, ps], writes=[yT])
    for hh in range(2):
        sl = slice(hh * 512, (hh + 1) * 512)
        for m in range(16):
            s_ = _rr(sq, m)
            S.op("pe", lambda m=m, sl=sl: nc.tensor.matmul(SUM.t[:], onesf.t[:], yT.t[:, m, sl], start=(m == 0), stop=(m == 15)), reads=[onesf, yT], writes=[SUM])
            S.op("act", lambda m=m, sl=sl, s_=s_: nc.scalar.activation(s_.t[:], yT.t[:, m, sl], AF.Square), reads=[yT], writes=[s_])
            S.op("pe", lambda m=m, s_=s_: nc.tensor.matmul(SSQ.t[:], onesf.t[:], s_.t[:], start=(m == 0), stop=(m == 15)), reads=[onesf, s_], writes=[SSQ])
        S.op("act", lambda: nc.scalar.activation(meanb.t[:], SUM.t[:], AF.Copy, scale=1.0 / DM), reads=[SUM], writes=[meanb])
        S.op("act", lambda: nc.scalar.activation(t1.t[:], SSQ.t[:], AF.Copy, scale=1.0 / DM), reads=[SSQ], writes=[t1])
        S.op("dve", lambda: nc.vector.tensor_tensor(t2.t[:], meanb.t[:], meanb.t[:], op=ALU.mult), reads=[meanb], writes=[t2])
        S.op("dve", lambda: nc.vector.tensor_tensor(t1.t[:], t1.t[:], t2.t[:], op=ALU.subtract), reads=[t1, t2], writes=[t1])
        S.op("dve", lambda: nc.vector.tensor_scalar(t1.t[:], t1.t[:], LN_EPS, None, op0=ALU.add), reads=[t1], writes=[t1])
        S.op("act", lambda: nc.scalar.activation(t2.t[:], t1.t[:], AF.Sqrt), reads=[t1], writes=[t2])
        S.op("dve", lambda: nc.vector.reciprocal(rstdb.t[:], t2.t[:]), reads=[t2], writes=[rstdb])
        for m in range(16):
            S.op("dve", lambda m=m, sl=sl: nc.vector.tensor_tensor(t1.t[:], yT.t[:, m, sl], meanb.t[:], op=ALU.subtract), reads=[yT, meanb], writes=[t1])
            S.op("dve", lambda: nc.vector.tensor_tensor(t2.t[:], t1.t[:], rstdb.t[:], op=ALU.mult), reads=[t1, rstdb], writes=[t2])
            S.op("dve", lambda m=m, sl=sl: nc.vector.tensor_scalar(yT.t[:, m, sl], t2.t[:], lng.t[:, m:m + 1], lnb.t[:, m:m + 1], op0=ALU.mult, op1=ALU.add),
                 reads=[t2, lng, lnb], writes=[yT])
            S.op("act", lambda m=m, sl=sl: nc.scalar.activation(ogt.t[:, m, sl], yT.t[:, m, sl], AF.Copy), reads=[yT], writes=[ogt])
    xlb = ogt
    for g in range(4):
        wb = _rr(wbufs, g)
        S.dma("pool", wb, wb.t[:], wg[:, 512 * g:512 * g + 512].rearrange("(c p) n -> p c n", p=128))
        for mi in range(4):
            m = 4 * g + mi
            for hh in range(2):
                sl = slice(hh * 512, (hh + 1) * 512)
                ps = _rr(psb, cnt)
                ps2 = _rr(psb, cnt + 1)
                cnt += 2

                def mm(ps=ps, wb=wb, mi=mi, sl=sl):
                    last = None
                    for k in range(16):
                        last = nc.tensor.matmul(ps.t[:], wb.t[:, k, mi * 128:(mi + 1) * 128], xlb.t[:, k, sl], start=(k == 0), stop=(k == 15))
                    return last
                S.op("pe", mm, reads=[wb, xlb], writes=[ps])
                S.op("act", lambda ps=ps: nc.scalar.activation(t1.t[:], ps.t[:], AF.Sigmoid), reads=[ps], writes=[t1])

                def mm2(ps2=ps2, m=m, sl=sl):
                    last = None
                    for k in range(2):
                        last = nc.tensor.matmul(ps2.t[:], wpl.t[:, k, m * 128:(m + 1) * 128], ptb.t[:, k, sl], start=(k == 0), stop=(k == 1))
                    return last
                S.op("pe", mm2, reads=[wpl, ptb], writes=[ps2])
                S.op("dve", lambda ps2=ps2: nc.vector.tensor_tensor(t2.t[:], ps2.t[:], t1.t[:], op=ALU.mult), reads=[ps2, t1], writes=[t2])
                S.op("dve", lambda m=m, sl=sl: nc.vector.tensor_tensor(yT.t[:, m, sl], yT.t[:, m, sl], t2.t[:], op=ALU.add), reads=[yT, t2], writes=[yT])
    S.dma("sp", rXO, XO.rearrange("(m p) t -> p m t", p=128), yT.t[:], reads=[yT])
    if with_proj:
        for m in range(16):
            S.op("act", lambda m=m: nc.scalar.activation(ogt.t[:, m, :], yT.t[:, m, :], AF.Copy), reads=[yT], writes=[ogt])
        QT, KT, SG, V = D["QT1"], D["KT1"], D["SG1"], D["V1"]
        rq, rk, rs, rv = [S.dram(a, n) for a, n in ((QT, "QT1"), (KT, "KT1"), (SG, "SG1"), (V, "V1"))]
        chunks = [("q1", QT[h * 128:(h + 1) * 128, :], rq) for h in range(16)] + [("k1", KT[n * 128:(n + 1) * 128, :], rk) for n in range(4)] + \
                 [("g", SG[h * 128:(h + 1) * 128, :], rs) for h in range(16)]
        emit_proj(S, nc, ogt, D["wB"], chunks, (36 * 128, V, rv, BF16), None, None, wbufs, stg, tmpf, psb)
    S.end_phase()


def emit_C(S, nc, D):
    S.begin_phase("C_")
    QT, SG, OGT = D["QT1"], D["SG1"], D["OGT1"]
    rq, rs = S.dram(QT, "QT1"), S.dram(SG, "SG1")
    rkg, rvg = S.dram(D["KTG1"], "KTG1"), S.dram(D["VG1"], "VG1")
    KTGv, VGv = _kv_views(D["KTG1"], D["VG1"])
    QTv = QT.rearrange("(h d) t -> d h t", d=128)
    SGv = SG.rearrange("(h d) t -> d h t", d=128)
    OGTv = OGT.rearrange("(h d) t -> d h t", d=128)
    rOGT = S.dram(OGT, "OGT1")
    utri = S.sbuf("utri", [128, 128], BF16)
    ones = S.sbuf("ones", [128, 128], BF16)
    S.dma("sp", utri, utri.t[:], D["utri"])
    S.dma("sp", ones, ones.t[:], D["ones_bf"])
    qj = S.sbuf("qj", [128, 16, 128], BF16)
    sgj = S.sbuf("sgj", [128, 16, 128], BF16)
    cmj = S.sbuf("cmj", [128, 8, 128], BF16)
    es = [S.sbuf("e%d" % i, [128, 4, 128], F32) for i in range(4)]
    Ls = [S.sbuf("L%d" % i, [128, 4, 128], BF16) for i in range(4)]
    gxs = [S.sbuf("gx%d" % i, [128, 4, 128], F32) for i in range(2)]
    As = [S.sbuf("A%d" % i, [128, 4, 128], BF16) for i in range(4)]
    Lsum = S.sbuf("Lsum", [128, 4, 128], BF16)
    kbufs = [S.sbuf("kbuf%d" % i, [128, 8, 128], BF16) for i in range(3)]
    vbufs = [S.sbuf("vbuf%d" % i, [128, 8, 128], BF16) for i in range(3)]
    ogt = S.sbuf("ogt", [128, 4, 128], BF16)
    Z = [S.psum("Z%d" % i, [128, 512], F32) for i in range(3)]
    E = [S.psum("E%d" % i, [128, 512], F32) for i in range(3)]
    OP = S.psum("OP", [128, 512], F32)
    st_ = {"kvc": 0}
    for j in range(NBL):
        tsl = slice(j * 128, (j + 1) * 128)
        S.dma("sp", qj, qj.t[:], QTv[:, :, tsl], reads=[rq])
        S.dma("sp", sgj, sgj.t[:], SGv[:, :, tsl], reads=[rs])
        S.dma("sp", cmj, cmj.t[:], D["CM"][:, j, :, :])
        for n in range(4):
            tiles = []
            for g in range(j, -1, -1):
                order = list(range(7, -1, -1)) if g % 2 == 0 else list(range(8))
                for r in order:
                    tiles.append((g, r))
            NT = len(tiles)
            kvb = {}
            tb = {}

            def load_piece(g, n=n):
                kbuf = _rr(kbufs, st_["kvc"])
                vbuf = _rr(vbufs, st_["kvc"])
                st_["kvc"] += 1
                kvb[g] = (kbuf, vbuf)
                S.dma("sp", kbuf, kbuf.t[:], KTGv[:, n, :, g * 128:(g + 1) * 128].rearrange("c d t -> d c t"), reads=[rkg])
                S.dma("sp", vbuf, vbuf.t[:], VGv[:, g, :, n, :].rearrange("c p d -> p c d"), reads=[rvg])

            def c_s0(i, n=n):
                g, r = tiles[i]
                if i % 8 == 0:
                    if i == 0:
                        load_piece(g)
                    if g - 1 >= 0:
                        load_piece(g - 1)
                kbuf, vbuf = kvb[g]
                z = _rr(Z, i)
                e = _rr(es, i)
                L = _rr(Ls, i)
                tb[i] = {"e": e, "L": L, "vbuf": vbuf}
                S.op("pe", lambda: nc.tensor.matmul(z.t[:], kbuf.t[:, r, :], qj.t[:, 4 * n:4 * n + 4, :], start=True, stop=True), reads=[kbuf, qj], writes=[z])
                S.op("act", lambda: nc.scalar.activation(e.t[:], z.t[:].rearrange("p (h t) -> p h t", h=4), AF.Exp), reads=[z], writes=[e])
                S.op("act", lambda: nc.scalar.activation(L.t[:], e.t[:], AF.Ln, bias=1.0), reads=[e], writes=[L])
                if g == j:
                    cmb = cmj.t[:, r, :].unsqueeze(1).to_broadcast([128, 4, 128])
                    S.op("dve", lambda: nc.vector.tensor_tensor(L.t[:], L.t[:], cmb, op=ALU.mult), reads=[L, cmj], writes=[L])

            def c_s1(i):
                g, r = tiles[i]
                first = (i == 0)
                t_ = tb[i]
                e, L = t_["e"], t_["L"]
                Eb = _rr(E, i)
                gx = _rr(gxs, i)
                A = _rr(As, i)
                t_["A"] = A
                S.op("pe", lambda: nc.tensor.matmul(Eb.t[:], utri.t[:], L.t[:].rearrange("p h t -> p (h t)"), start=True, stop=first), reads=[utri, L], writes=[Eb])
                if not first:
                    S.op("pe", lambda: nc.tensor.matmul(Eb.t[:], ones.t[:], Lsum.t[:].rearrange("p h t -> p (h t)"), start=False, stop=True),
                         reads=[ones, Lsum], writes=[Eb])
                if first:
                    S.op("pool", lambda: nc.gpsimd.tensor_copy(Lsum.t[:], L.t[:]), reads=[L], writes=[Lsum])
                else:
                    S.op("pool", lambda: nc.gpsimd.tensor_tensor(Lsum.t[:], Lsum.t[:], L.t[:], op=ALU.add), reads=[Lsum, L], writes=[Lsum])
                S.op("act", lambda: nc.scalar.activation(gx.t[:], Eb.t[:].rearrange("p (h t) -> p h t", h=4), AF.Exp, scale=-1.0), reads=[Eb], writes=[gx])
                S.op("dve", lambda: nc.vector.tensor_tensor(A.t[:], e.t[:], gx.t[:], op=ALU.mult), reads=[e, gx], writes=[A])
                if g == j:
                    cmb = cmj.t[:, r, :].unsqueeze(1).to_broadcast([128, 4, 128])
                    S.op("dve", lambda: nc.vector.tensor_tensor(A.t[:], A.t[:], cmb, op=ALU.mult), reads=[A, cmj], writes=[A])

            def c_s2(i):
                g, r = tiles[i]
                t_ = tb.pop(i)
                A, vbuf = t_["A"], t_["vbuf"]
                S.op("pe", lambda: nc.tensor.matmul(OP.t[:], vbuf.t[:, r, :], A.t[:].rearrange("p h t -> p (h t)"), start=(i == 0), stop=(i == NT - 1)),
                     reads=[vbuf, A], writes=[OP])

            _pipeline(NT, [(0, c_s0), (2, c_s1), (4, c_s2)])
            S.op("dve", lambda n=n: nc.vector.tensor_tensor(ogt.t[:], OP.t[:].rearrange("p (h t) -> p h t", h=4), sgj.t[:, 4 * n:4 * n + 4, :], op=ALU.mult),
                 reads=[OP, sgj], writes=[ogt])
            S.dma("sp", rOGT, OGTv[:, 4 * n:4 * n + 4, tsl], ogt.t[:], reads=[ogt])
    S.end_phase()


def host_pen(c):
    p = np.arange(128)[:, None, None, None]
    sl = np.arange(8)[None, None, :, None]
    s = np.arange(128)[None, None, None, :]
    bq = np.array([blk(c, jj) for jj in range(NBL)])[None, :, None, None]
    bk = np.array([[blk(cc, jj) for cc in range(8)] for jj in range(NBL)])[None, :, :, None]
    pen = np.where(128 * bk + s <= 128 * bq + p, 0.0, NEG).astype(np.float32)
    return np.ascontiguousarray(pen.reshape(128, NBL, 1024))


def host_cm(c):
    s = np.arange(128)[:, None, None, None]
    t = np.arange(128)[None, None, None, :]
    bq = np.array([blk(c, jj) for jj in range(NBL)])[None, :, None, None]
    bk = np.array([[blk(cc, jj) for cc in range(8)] for jj in range(NBL)])[None, :, :, None]
    return _bf((128 * bk + s < 128 * bq + t).astype(np.float32))


def build_fused():
    nc = bass.Bass("TRN2", target_bir_lowering=False)
    D = {}

    def ein(name, shape, dt):
        D[name] = nc.dram_tensor(name, shape, dt, kind="ExternalInput").ap()

    def internal(name, shape, dt):
        D[name] = nc.dram_tensor(name, shape, dt, kind="Internal").ap()

    ein("xT", [DM, TOK], F32)
    ein("pos", [1, TOK], I32)
    ein("wA", [DM, 45 * 128 + 512 + 16], F32)
    ein("wB", [DM, 5120], F32)
    ein("ropec", [128, 4], F32)
    for nm in ("perm_qk", "perm_i", "ident", "ones_bf", "utri"):
        ein(nm, [128, 128], BF16)
    ein("ones_f", [128, 128], F32)
    ein("PEN", [128, NBL, 1024], F32)
    ein("CM", [128, NBL, 8, 128], BF16)
    for L in ("0", "1"):
        ein("wo" + L, [DM, DM], F32)
        ein("wg" + L, [DM, DM], F32)
        ein("wple" + L, [256, DM], F32)
        ein("pT" + L, [256, TOK], F32)
        ein("lng" + L, [128, 16], F32)
        ein("lnb" + L, [128, 16], F32)
    for L in ("0", "1"):
        internal("QT" + L, [2048, TOK], BF16)
        internal("KT" + L, [512, TOK], BF16)
        internal("SG" + L, [2048, TOK], BF16)
        internal("V" + L, [TOK, 512], BF16)
        internal("KTG" + L, [NCORES * 512, TOK], BF16)
        internal("VG" + L, [NCORES * TOK, 512], BF16)
        internal("OGT" + L, [2048, TOK], BF16)
    internal("IQT0", [1024, TOK], BF16)
    internal("IK0", [128, TOK], BF16)
    internal("IKG0", [NCORES * 128, TOK], BF16)
    internal("IW0", [TOK, 16], F32)
    internal("X1", [DM, TOK], F32)
    D["OUT"] = nc.dram_tensor("OUT", [DM, TOK], F32, kind="ExternalOutput").ap()
    D["xin0"], D["xout0"], D["xin1"], D["xout1"] = D["xT"], D["X1"], D["X1"], D["OUT"]
    S = Sched(nc)
    emit_A(S, nc, D)
    for src, dst in (("KT0", "KTG0"), ("V0", "VG0"), ("IK0", "IKG0")):
        S.collective("AllGather", S.dram(D[dst], dst), D[dst], D[src], reads=[S.dram(D[src], src)])
    S.barrier()
    emit_B(S, nc, D)
    emit_P(S, nc, D, 0, True)
    for src, dst in (("KT1", "KTG1"), ("V1", "VG1")):
        S.collective("AllGather", S.dram(D[dst], dst), D[dst], D[src], reads=[S.dram(D[src], src)])
    S.barrier()
    emit_C(S, nc, D)
    emit_P(S, nc, D, 1, False)
    S.finish()
    return nc


def kernel(**inputs):
    cs = host_consts()
    x = np.asarray(inputs["x"])[0]
    pos = np.asarray(inputs["positions"]).astype(np.int32)
    wA = host_wA(np.asarray(inputs["w_in_a"]))
    wib = np.asarray(inputs["w_in_b"])[0]
    wkv = np.asarray(inputs["w_kv_b"])
    wB = np.ascontiguousarray(np.concatenate([wib[:, :2048], wkv[:, :512], wib[:, 2048:], wkv[:, 512:]], axis=1))
    p = np.asarray(inputs["p"])
    shared = {"wA": wA, "wB": wB, "ropec": cs["ropec"], "perm_qk": cs["perm_qk"], "perm_i": cs["perm_i"], "ident": cs["ident"],
              "ones_bf": cs["ones_bf"], "utri": cs["utri"], "ones_f": cs["ones_f"]}
    for L, wo in ((0, "w_o_a"), (1, "w_o_b")):
        shared["wo%d" % L] = np.ascontiguousarray(np.asarray(inputs[wo])[0])
        shared["wg%d" % L] = np.ascontiguousarray(np.asarray(inputs["w_ple_gate"])[L])
        shared["wple%d" % L] = np.ascontiguousarray(np.asarray(inputs["w_ple"])[L])
        shared["lng%d" % L] = np.ascontiguousarray(np.asarray(inputs["ln_g"])[L].reshape(16, 128).T)
        shared["lnb%d" % L] = np.ascontiguousarray(np.asarray(inputs["ln_b"])[L].reshape(16, 128).T)
    ims = []
    for c in range(NCORES):
        tok = host_tokens(c)
        im = dict(shared)
        im["xT"] = np.ascontiguousarray(x[tok].T)
        im["pos"] = np.ascontiguousarray(pos[:, tok])
        im["PEN"] = host_pen(c)
        im["CM"] = host_cm(c)
        for L in (0, 1):
            im["pT%d" % L] = np.ascontiguousarray(p[L, 0][tok].T)
        ims.append(im)
    res = run_bass_kernel_spmd(build_fused(), ims, core_ids=list(range(NCORES))).results
    out = np.zeros((1, SEQ, DM), np.float32)
    for c in range(NCORES):
        out[0, host_tokens(c), :] = np.asarray(res[c]["OUT"]).T
    return out
```

```python
import numpy as np
from contextlib import ExitStack
import concourse.bass as bass
import concourse.mybir as mybir
from concourse.bass_utils import run_bass_kernel_spmd
import ml_dtypes

F32 = mybir.dt.float32
BF16 = mybir.dt.bfloat16
I32 = mybir.dt.int32
AF = mybir.ActivationFunctionType
ALU = mybir.AluOpType
AX = mybir.AxisListType


class _Ctr:
    def __init__(self, sem, name):
        self.sem = sem
        self.count = 0
        self.name = name


class _Res:
    def __init__(self, name, t=None):
        self.name = name
        self.t = t
        self.lw = None
        self.rd = {}
        self.dsem = None


class _Eng:
    def __init__(self, name, h, ctr):
        self.name = name
        self.h = h
        self.ctr = ctr
        self.seen = {}


class Sched:
    def __init__(self, nc):
        self.nc = nc
        self.es = ExitStack()
        self.eng = {}
        self.cc_sems = [self.es.enter_context(nc.semaphore("cc%d" % i)) for i in range(6)]
        for name, h in (("pe", nc.tensor), ("act", nc.scalar), ("dve", nc.vector),
                        ("pool", nc.gpsimd), ("sp", nc.sync)):
            sem = self.es.enter_context(nc.semaphore("sem_" + name))
            self.eng[name] = _Eng(name, h, _Ctr(sem, name))
        self.dma_res = []
        self.pes = None
        self.prefix = ""
        self.dreg = {}

    def begin_phase(self, prefix):
        self.pes = ExitStack()
        self.prefix = prefix

    def end_phase(self):
        self.barrier()
        self.pes.close()
        self.pes = None

    def barrier(self):
        ctrs = [e.ctr for e in self.eng.values()] + [r.dsem for r in self.dma_res]
        for E in self.eng.values():
            for c in ctrs:
                if c is E.ctr or c.count == 0:
                    continue
                if E.seen.get(c, 0) < c.count:
                    E.h.wait_ge(c.sem, c.count)
                    E.seen[c] = c.count

    def sbuf(self, name, shape, dtype):
        st = self.pes if self.pes is not None else self.es
        t = st.enter_context(self.nc.sbuf_tensor("sb_" + self.prefix + name, shape, dtype))
        return _Res(self.prefix + name, t)

    def psum(self, name, shape, dtype):
        st = self.pes if self.pes is not None else self.es
        t = st.enter_context(self.nc.psum_tensor("pp_" + self.prefix + name, shape, dtype))
        return _Res(self.prefix + name, t)

    def dram(self, ap, name):
        if name not in self.dreg:
            self.dreg[name] = _Res(name, ap)
        return self.dreg[name]

    def view(self, name):
        return _Res(name, None)

    def _dsem(self, r):
        if r.dsem is None:
            sem = self.es.enter_context(self.nc.semaphore("dsem_" + r.name))
            r.dsem = _Ctr(sem, r.name)
            self.dma_res.append(r)
        return r.dsem

    def _waits(self, E, reads, writes):
        need = {}

        def req(kv, raw):
            if kv is None:
                return
            key, val = kv
            if key is E.ctr:
                if E.name in ("pe", "sp"):
                    return
            if need.get(key, 0) < val:
                need[key] = val

        for r in reads:
            req(r.lw, True)
        for w in writes:
            req(w.lw, False)
            for k, v in w.rd.items():
                req((k, v), False)
        for key, val in need.items():
            if E.seen.get(key, 0) >= val:
                continue
            E.h.wait_ge(key.sem, val)
            E.seen[key] = val

    def op(self, eng, fn, reads=(), writes=()):
        E = self.eng[eng]
        self._waits(E, reads, writes)
        inst = fn()
        E.ctr.count += 1
        inst.then_inc(E.ctr.sem, 1)
        for w in writes:
            w.lw = (E.ctr, E.ctr.count)
            w.rd = {}
        for r in reads:
            if r not in writes:
                r.rd[E.ctr] = E.ctr.count
        return inst

    def dma(self, queue, dst, out_ap, in_ap, reads=(), cast=False):
        E = self.eng[queue]
        self._waits(E, reads, [dst])
        c = self._dsem(dst)
        inst = E.h.dma_start(out=out_ap, in_=in_ap)
        inst.then_inc(c.sem, 16)
        c.count += 16
        dst.lw = (c, c.count)
        dst.rd = {}
        for r in reads:
            r.rd[c] = c.count
        return inst

    def collective(self, kind, dst, out_ap, in_ap, reads=()):
        E = self.eng["pool"]
        self._waits(E, reads, [dst])
        if dst.dsem is None:
            dst.dsem = _Ctr(self.cc_sems.pop(0), dst.name)
            self.dma_res.append(dst)
        c = dst.dsem
        inst = E.h.collective_compute(kind, ALU.bypass, replica_groups=[list(range(NCORES))], ins=[in_ap], outs=[out_ap])
        inst.then_inc(c.sem, 1)
        c.count += 1
        dst.lw = (c, c.count)
        dst.rd = {}
        for r in reads:
            r.rd[c] = c.count
        return inst

    def finish(self):
        sp = self.eng["sp"]
        for r in self.dma_res:
            if r.dsem.count > 0 and sp.seen.get(r.dsem, 0) < r.dsem.count:
                sp.h.wait_ge(r.dsem.sem, r.dsem.count)
        for name in ("pe", "act", "dve", "pool"):
            c = self.eng[name].ctr
            if c.count > 0:
                sp.h.wait_ge(c.sem, c.count)
        self.es.close()


NCORES = 8
NBL = 8
TOK = 1024
DM = 2048
SEQ = 8192
THETA = 500000.0
ALPHA = 4.0 ** 0.25
LN_EPS = 1e-5
NEG = -1.0e30
PI = float(np.pi)
C1 = 6.28125
C2 = float(2.0 * np.pi - 6.28125)
NBIS = 24


def blk(c, j):
    return 16 * (j // 2) + (c if j % 2 == 0 else 15 - c)


def _rr(lst, i):
    return lst[i % len(lst)]


def emit_rope_tables(S, nc, pos_ap, ropec, ci, cosT, sinT, tmp):
    posi, posf, ang, kf, ki, r, m = tmp
    S.dma("sp", posi, posi.t[:], pos_ap.partition_broadcast(128))
    S.op("dve", lambda: nc.vector.tensor_copy(posf.t[:], posi.t[:]), reads=[posi], writes=[posf])
    for which, dst in ((0, sinT), (1, cosT)):
        if which == 0:
            S.op("dve", lambda: nc.vector.tensor_scalar(ang.t[:], posf.t[:], ropec.t[:, ci:ci + 1], None, op0=ALU.mult),
                 reads=[posf, ropec], writes=[ang])
        else:
            S.op("dve", lambda: nc.vector.tensor_scalar(ang.t[:], posf.t[:], ropec.t[:, ci:ci + 1], PI / 2, op0=ALU.mult, op1=ALU.add),
                 reads=[posf, ropec], writes=[ang])
        S.op("dve", lambda: nc.vector.tensor_scalar(ki.t[:], ang.t[:], 1.0 / (2 * PI), None, op0=ALU.mult), reads=[ang], writes=[ki])
        S.op("dve", lambda: nc.vector.tensor_copy(kf.t[:], ki.t[:]), reads=[ki], writes=[kf])
        S.op("dve", lambda: nc.vector.scalar_tensor_tensor(r.t[:], kf.t[:], -C1, ang.t[:], op0=ALU.mult, op1=ALU.add),
             reads=[kf, ang], writes=[r])
        S.op("dve", lambda: nc.vector.scalar_tensor_tensor(ang.t[:], kf.t[:], -C2, r.t[:], op0=ALU.mult, op1=ALU.add),
             reads=[kf, r], writes=[ang])
        S.op("dve", lambda: nc.vector.tensor_scalar(m.t[:], ang.t[:], PI, -2 * PI, op0=ALU.is_gt, op1=ALU.mult), reads=[ang], writes=[m])
        S.op("dve", lambda: nc.vector.tensor_tensor(r.t[:], ang.t[:], m.t[:], op=ALU.add), reads=[ang, m], writes=[r])
        S.op("dve", lambda: nc.vector.tensor_scalar(m.t[:], r.t[:], -PI, 2 * PI, op0=ALU.is_lt, op1=ALU.mult), reads=[r], writes=[m])
        S.op("dve", lambda: nc.vector.tensor_tensor(ang.t[:], r.t[:], m.t[:], op=ALU.add), reads=[r, m], writes=[ang])
        S.op("dve", lambda: nc.vector.tensor_scalar(r.t[:], ang.t[:], PI, -PI, op0=ALU.min, op1=ALU.max), reads=[ang], writes=[r])
        S.op("act", lambda: nc.scalar.activation(dst.t[:], r.t[:], AF.Sin), reads=[r], writes=[dst])
        if which == 0:
            S.op("dve", lambda: nc.vector.tensor_scalar(dst.t[:], dst.t[:], ropec.t[:, ci + 1:ci + 2], None, op0=ALU.mult),
                 reads=[dst, ropec], writes=[dst])


def emit_proj(S, nc, xb, w_ap, chunks, v_spec, iw_spec, rope, wbufs, stg, tmpf, psb, col_base=0):
    ngroups = (len(chunks) + 3) // 4
    cnt = 0
    for g in range(ngroups):
        wb = _rr(wbufs, g)
        gch = chunks[4 * g:4 * g + 4]
        ncol = 128 * len(gch)
        S.dma("pool", wb, wb.t[:, :, 0:ncol], w_ap[:, col_base + 512 * g:col_base + 512 * g + ncol].rearrange("(c p) n -> p c n", p=128))
        for ci, (kind, oap, ores) in enumerate(gch):
            for hh in range(2):
                ps = _rr(psb, cnt)
                st = _rr(stg, cnt)
                cnt += 1

                def mm(ps=ps, wb=wb, ci=ci, hh=hh):
                    last = None
                    for k in range(16):
                        last = nc.tensor.matmul(ps.t[:], wb.t[:, k, ci * 128:(ci + 1) * 128], xb.t[:, k, hh * 512:(hh + 1) * 512],
                                                start=(k == 0), stop=(k == 15))
                    return last
                S.op("pe", mm, reads=[wb, xb], writes=[ps])
                sl = slice(hh * 512, (hh + 1) * 512)
                if kind == "g":
                    S.op("act", lambda ps=ps, st=st: nc.scalar.activation(st.t[:], ps.t[:], AF.Silu), reads=[ps], writes=[st])
                else:
                    sc = (128.0 ** -0.5) if kind in ("q", "q1") else 1.0
                    S.op("act", lambda ps=ps, st=st, sc=sc: nc.scalar.activation(st.t[:], ps.t[:], AF.Copy, scale=sc), reads=[ps], writes=[st])
                    if kind in ("q", "k", "iq"):
                        perm, cosT, sinT = rope["qk"] if kind in ("q", "k") else rope["i"]
                        ps2 = _rr(psb, cnt)
                        cnt += 1
                        t1, t2 = tmpf
                        S.op("pe", lambda ps2=ps2, st=st, perm=perm: nc.tensor.matmul(ps2.t[:], perm.t[:], st.t[:], start=True, stop=True),
                             reads=[perm, st], writes=[ps2])
                        S.op("dve", lambda st=st, cosT=cosT, sl=sl: nc.vector.tensor_tensor(t1.t[:], st.t[:], cosT.t[:, sl], op=ALU.mult),
                             reads=[st, cosT], writes=[t1])
                        S.op("dve", lambda ps2=ps2, sinT=sinT, sl=sl: nc.vector.tensor_tensor(t2.t[:], ps2.t[:], sinT.t[:, sl], op=ALU.mult),
                             reads=[ps2, sinT], writes=[t2])
                        S.op("dve", lambda st=st: nc.vector.tensor_tensor(st.t[:], t1.t[:], t2.t[:], op=ALU.add), reads=[t1, t2], writes=[st])
                S.dma("sp", ores, oap[:, sl], st.t[:], reads=[st])
    for spec, width in ((v_spec, 512), (iw_spec, 16)):
        if spec is None:
            continue
        col0, oap, ores, odt = spec
        wb = _rr(wbufs, ngroups + (0 if width == 512 else 1))
        S.dma("pool", wb, wb.t[:, :, 0:width], w_ap[:, col0:col0 + width].rearrange("(c p) n -> p c n", p=128))
        for tt in range(8):
            ps = _rr(psb, cnt)
            cnt += 1

            def mm(ps=ps, wb=wb, tt=tt, width=width):
                last = None
                for k in range(16):
                    last = nc.tensor.matmul(ps.t[:, 0:width], xb.t[:, k, tt * 128:(tt + 1) * 128], wb.t[:, k, 0:width],
                                            start=(k == 0), stop=(k == 15))
                return last
            S.op("pe", mm, reads=[wb, xb], writes=[ps])
            if width == 512:
                st = _rr(stg, cnt)
                S.op("act", lambda ps=ps, st=st: nc.scalar.activation(st.t[:], ps.t[:], AF.Copy), reads=[ps], writes=[st])
                S.dma("sp", ores, oap[tt * 128:(tt + 1) * 128, :], st.t[:], reads=[st])
            else:
                t1 = tmpf[0]
                S.op("act", lambda ps=ps, t1=t1: nc.scalar.activation(t1.t[:, 0:16], ps.t[:, 0:16], AF.Copy), reads=[ps], writes=[t1])
                S.dma("sp", ores, oap[tt * 128:(tt + 1) * 128, :], t1.t[:, 0:16], reads=[t1])


def emit_A(S, nc, D):
    S.begin_phase("A_")
    xb = S.sbuf("xb", [128, 16, TOK], BF16)
    S.dma("pool", xb, xb.t[:], D["xT"].rearrange("(c p) t -> p c t", p=128))
    ropec = S.sbuf("ropec", [128, 4], F32)
    pqk = S.sbuf("pqk", [128, 128], BF16)
    pii = S.sbuf("pii", [128, 128], BF16)
    S.dma("sp", ropec, ropec.t[:], D["ropec"])
    S.dma("sp", pqk, pqk.t[:], D["perm_qk"])
    S.dma("sp", pii, pii.t[:], D["perm_i"])
    tabs = [S.sbuf("tab%d" % i, [128, TOK], F32) for i in range(4)]
    posi = S.sbuf("posi", [128, TOK], I32)
    ki = S.sbuf("ki", [128, TOK], I32)
    tmp = [posi] + [S.sbuf("rt%d" % i, [128, TOK], F32) for i in range(3)] + [ki] + [S.sbuf("rt%d" % i, [128, TOK], F32) for i in range(3, 5)]
    emit_rope_tables(S, nc, D["pos"], ropec, 0, tabs[0], tabs[1], tmp)
    emit_rope_tables(S, nc, D["pos"], ropec, 2, tabs[2], tabs[3], tmp)
    rope = {"qk": (pqk, tabs[0], tabs[1]), "i": (pii, tabs[2], tabs[3])}
    wbufs = [S.sbuf("wb%d" % i, [128, 16, 512], BF16) for i in range(2)]
    stg = [S.sbuf("stg%d" % i, [128, 512], BF16) for i in range(4)]
    tmpf = [S.sbuf("tmpf%d" % i, [128, 512], F32) for i in range(2)]
    psb = [S.psum("ps%d" % i, [128, 512], F32) for i in range(4)]
    QT, KT, SG, IQT, IKT, V, IW = D["QT0"], D["KT0"], D["SG0"], D["IQT0"], D["IK0"], D["V0"], D["IW0"]
    rq, rk, rs, riq, rik, rv, riw = [S.dram(a, n) for a, n in ((QT, "QT0"), (KT, "KT0"), (SG, "SG0"), (IQT, "IQT0"), (IKT, "IK0"), (V, "V0"), (IW, "IW0"))]
    chunks1 = [("k", KT[n * 128:(n + 1) * 128, :], rk) for n in range(4)] + [("iq", IKT, rik)]
    emit_proj(S, nc, xb, D["wA"], chunks1, (640, V, rv, BF16), None, rope, wbufs, stg, tmpf, psb)
    for src, dst in (("KT0", "KTG0"), ("V0", "VG0"), ("IK0", "IKG0")):
        S.collective("AllGather", S.dram(D[dst], dst), D[dst], D[src], reads=[S.dram(D[src], src)])
    chunks2 = [("q", QT[h * 128:(h + 1) * 128, :], rq) for h in range(16)] + [("g", SG[h * 128:(h + 1) * 128, :], rs) for h in range(16)] + \
              [("iq", IQT[h * 128:(h + 1) * 128, :], riq) for h in range(8)]
    emit_proj(S, nc, xb, D["wA"], chunks2, None, (1152 + 40 * 128, IW, riw, F32), rope, wbufs, stg, tmpf, psb, col_base=1152)
    S.end_phase()


def _pipeline(n_items, stages):
    maxlag = max(l for l, _ in stages)
    for step in range(n_items + maxlag):
        for lag, fn in stages:
            i = step - lag
            if 0 <= i < n_items:
                fn(i)


def _kv_views(KTG, VG):
    KTGv = KTG.rearrange("(c n d) t -> c n d t", c=NCORES, n=4, d=128)
    VGv = VG.rearrange("(c j p) (n d) -> c j p n d", c=NCORES, j=NBL, p=128, n=4)
    return KTGv, VGv


def emit_B(S, nc, D):
    S.begin_phase("B_")
    QT, SG, IQT, IWd, OGT = D["QT0"], D["SG0"], D["IQT0"], D["IW0"], D["OGT0"]
    rq, rs, riq, riw = S.dram(QT, "QT0"), S.dram(SG, "SG0"), S.dram(IQT, "IQT0"), S.dram(IWd, "IW0")
    rkg, rvg, rig = S.dram(D["KTG0"], "KTG0"), S.dram(D["VG0"], "VG0"), S.dram(D["IKG0"], "IKG0")
    KTGv, VGv = _kv_views(D["KTG0"], D["VG0"])
    IKGv = D["IKG0"].rearrange("(c p) t -> c p t", c=NCORES)
    QTv = QT.rearrange("(h d) t -> d h t", d=128)
    SGv = SG.rearrange("(h d) t -> d h t", d=128)
    IQv = IQT.rearrange("(h d) t -> d h t", d=128)
    OGTv = OGT.rearrange("(h d) t -> d h t", d=128)
    rOGT = S.dram(OGT, "OGT0")
    ident = S.sbuf("ident", [128, 128], BF16)
    ones = S.sbuf("ones", [128, 128], BF16)
    ikt = S.sbuf("ikt", [128, 64, 128], BF16)
    S.dma("sp", ident, ident.t[:], D["ident"])
    S.dma("sp", ones, ones.t[:], D["ones_bf"])
    for g in range(NBL):
        S.dma("sp", ikt, ikt.t[:, 8 * g:8 * g + 8, :], IKGv[:, :, g * 128:(g + 1) * 128].rearrange("c p t -> p c t"), reads=[rig])
    qj = S.sbuf("qj", [128, 16, 128], BF16)
    sgj = S.sbuf("sgj", [128, 16, 128], BF16)
    iqjs = [S.sbuf("iqj%d" % i, [128, 8, 128], BF16) for i in range(2)]
    iwjs = [S.sbuf("iwj%d" % i, [128, 16], F32) for i in range(2)]
    penjs = [S.sbuf("penj%d" % i, [128, 1024], F32) for i in range(2)]
    absws = [S.sbuf("absw%d" % i, [128, 16], F32) for i in range(2)]
    sgns = [S.sbuf("sgn%d" % i, [128, 16], F32) for i in range(2)]
    Dgs = [S.sbuf("Dg%d" % i, [128, 16, 128], BF16) for i in range(2)]
    iscs = [S.sbuf("isc%d" % i, [128, SEQ], F32) for i in range(2)]
    Mb = S.sbuf("Mb", [128, SEQ], BF16)
    MT = S.sbuf("MT", [128, SEQ], BF16)
    pw = S.sbuf("pw", [128, NBIS], F32)
    halfs = S.sbuf("halfs", [128, NBIS], F32)
    lo = S.sbuf("lo", [128, 1], F32)
    hi = S.sbuf("hi", [128, 1], F32)
    rng = S.sbuf("rng", [128, 1], F32)
    mid = S.sbuf("mid", [128, 1], F32)
    cntt = S.sbuf("cntt", [128, 1], F32)
    step = S.sbuf("step", [128, 1], F32)
    rbs = [S.sbuf("rb%d" % i, [128, 512], BF16) for i in range(4)]
    pts = [S.sbuf("pt%d" % i, [128, 4, 128], BF16) for i in range(4)]
    pms = [S.sbuf("pm%d" % i, [128, 4, 128], BF16) for i in range(4)]
    kbufs = [S.sbuf("kbuf%d" % i, [128, 8, 128], BF16) for i in range(4)]
    vbufs = [S.sbuf("vbuf%d" % i, [128, 8, 128], BF16) for i in range(4)]
    rden = S.sbuf("rden", [128, 512], F32)
    onrm = S.sbuf("onrm", [128, 4, 128], F32)
    ogt = S.sbuf("ogt", [128, 4, 128], BF16)
    W = [S.psum("W%d" % i, [128, 512], F32) for i in range(4)]
    ISC = S.psum("ISC", [128, 512], F32)
    TP = S.psum("TP", [128, 1024], BF16)
    OP = S.psum("OP", [128, 512], F32)
    DEN = S.psum("DEN", [128, 512], F32)
    for k in range(NBIS):
        S.op("dve", lambda k=k: nc.vector.memset(pw.t[:, k:k + 1], 2.0 ** -(k + 1)), writes=[pw])
    st_ = {"wc": 0, "kvc": 0}

    kvpre = {}

    def ensure_piece(jj, nn, gg):
        if (jj, nn, gg) in kvpre:
            return
        kbuf = _rr(kbufs, st_["kvc"])
        vbuf = _rr(vbufs, st_["kvc"])
        st_["kvc"] += 1
        kvpre[(jj, nn, gg)] = (kbuf, vbuf)
        S.dma("sp", kbuf, kbuf.t[:], KTGv[:, nn, :, gg * 128:(gg + 1) * 128].rearrange("c d t -> d c t"), reads=[rkg])
        S.dma("act", vbuf, vbuf.t[:], VGv[:, gg, :, nn, :].rearrange("c p d -> p c d"), reads=[rvg])

    def prep_idx(j):
        tsl = slice(j * 128, (j + 1) * 128)
        iqj, iwj, penj, absw, sgn, Dg = iqjs[j % 2], iwjs[j % 2], penjs[j % 2], absws[j % 2], sgns[j % 2], Dgs[j % 2]
        S.dma("sp", iqj, iqj.t[:], IQv[:, :, tsl], reads=[riq])
        S.dma("sp", iwj, iwj.t[:], IWd[tsl, :], reads=[riw])
        S.dma("sp", penj, penj.t[:], D["PEN"][:, j, :])
        S.op("act", lambda: nc.scalar.activation(absw.t[:], iwj.t[:], AF.Abs), reads=[iwj], writes=[absw])
        S.op("dve", lambda: nc.vector.tensor_scalar(sgn.t[:], iwj.t[:], 0.0, 2.0, op0=ALU.is_ge, op1=ALU.mult), reads=[iwj], writes=[sgn])
        S.op("dve", lambda: nc.vector.tensor_scalar(sgn.t[:], sgn.t[:], -1.0, None, op0=ALU.add), reads=[sgn], writes=[sgn])
        for h in range(16):
            S.op("dve", lambda h=h: nc.vector.tensor_scalar(Dg.t[:, h, :], ident.t[:], sgn.t[:, h:h + 1], None, op0=ALU.mult),
                 reads=[ident, sgn], writes=[Dg])

    def idx(j):
        Sj = 1024 * (j + 1)
        iqj, absw, Dg, isc = iqjs[j % 2], absws[j % 2], Dgs[j % 2], iscs[j % 2]
        items = [(c, hp) for c in range(Sj // 512) for hp in range(8)]
        dbuf = {}

        def ix_s0(i):
            c, hp = items[i]
            pair = []
            for hh in range(2):
                h = 2 * hp + hh
                pb = 64 * hh
                d = _rr(W, st_["wc"])
                st_["wc"] += 1
                rb = _rr(rbs, 2 * i + hh)
                pair.append((h, d, rb))
                S.op("pe", lambda d=d, pb=pb: nc.tensor.matmul(d.t[:], iqj.t[pb:pb + 64, hp, :], ikt.t[pb:pb + 64, 4 * c:4 * c + 4, :],
                                                            start=True, stop=True), reads=[iqj, ikt], writes=[d])
            for h, d, rb in pair:
                S.op("act", lambda h=h, d=d, rb=rb: nc.scalar.activation(rb.t[:], d.t[:], AF.Relu, scale=absw.t[:, h:h + 1]), reads=[d, absw], writes=[rb])
            dbuf[i] = pair

        def ix_s1(i):
            c, hp = items[i]
            for h, d, rb in dbuf.pop(i):
                S.op("pe", lambda h=h, rb=rb: nc.tensor.matmul(ISC.t[:], Dg.t[:, h, :], rb.t[:], start=(h == 0), stop=(h == 15)), reads=[Dg, rb], writes=[ISC])
                if h == 15:
                    S.op("act", lambda: nc.scalar.activation(isc.t[:, c * 512:(c + 1) * 512], ISC.t[:], AF.Copy), reads=[ISC], writes=[isc])

        _pipeline(len(items), [(0, ix_s0), (1, ix_s1)])

    def bis(j):
        Sj = 1024 * (j + 1)
        isc, penj = iscs[j % 2], penjs[j % 2]
        S.op("dve", lambda: nc.vector.tensor_reduce(lo.t[:], isc.t[:, 0:Sj], axis=AX.X, op=ALU.min), reads=[isc], writes=[lo])
        S.op("dve", lambda: nc.vector.tensor_tensor(isc.t[:, Sj - 1024:Sj], isc.t[:, Sj - 1024:Sj], penj.t[:], op=ALU.add), reads=[isc, penj], writes=[isc])
        S.op("dve", lambda: nc.vector.tensor_reduce(hi.t[:], isc.t[:, 0:Sj], axis=AX.X, op=ALU.max), reads=[isc], writes=[hi])
        S.op("dve", lambda: nc.vector.tensor_tensor(rng.t[:], hi.t[:], lo.t[:], op=ALU.subtract), reads=[hi, lo], writes=[rng])
        S.op("dve", lambda: nc.vector.tensor_scalar(halfs.t[:], pw.t[:], rng.t[:, 0:1], None, op0=ALU.mult), reads=[pw, rng], writes=[halfs])
        for k in range(NBIS):
            S.op("dve", lambda k=k: nc.vector.tensor_tensor(mid.t[:], lo.t[:], halfs.t[:, k:k + 1], op=ALU.add), reads=[lo, halfs], writes=[mid])
            S.op("dve", lambda: nc.vector.tensor_scalar(Mb.t[:, 0:Sj], isc.t[:, 0:Sj], mid.t[:, 0:1], None, op0=ALU.is_ge, op1=ALU.add, accum_out=cntt.t[:]),
                 reads=[isc, mid], writes=[Mb, cntt])
            S.op("dve", lambda k=k: nc.vector.scalar_tensor_tensor(step.t[:], cntt.t[:], 255.5, halfs.t[:, k:k + 1], op0=ALU.is_ge, op1=ALU.mult),
                 reads=[cntt, halfs], writes=[step])
            S.op("dve", lambda: nc.vector.tensor_tensor(lo.t[:], lo.t[:], step.t[:], op=ALU.add), reads=[lo, step], writes=[lo])
        S.op("dve", lambda: nc.vector.tensor_scalar(Mb.t[:, 0:Sj], isc.t[:, 0:Sj], lo.t[:, 0:1], None, op0=ALU.is_ge), reads=[isc, lo], writes=[Mb])
        for g in range(Sj // 1024):
            for r in range(8):
                kb = 8 * g + r
                S.op("pe", lambda r=r, kb=kb: nc.tensor.transpose(TP.t[:, r * 128:(r + 1) * 128], Mb.t[:, kb * 128:(kb + 1) * 128], ident.t[:]),
                     reads=[Mb, ident], writes=[TP])
            if g % 2 == 0:
                S.op("act", lambda g=g: nc.scalar.activation(MT.t[:, g * 1024:(g + 1) * 1024], TP.t[:], AF.Copy), reads=[TP], writes=[MT])
            else:
                S.op("dve", lambda g=g: nc.vector.tensor_copy(MT.t[:, g * 1024:(g + 1) * 1024], TP.t[:]), reads=[TP], writes=[MT])

    def att(j):
        Sj = 1024 * (j + 1)
        tsl = slice(j * 128, (j + 1) * 128)
        nkb = Sj // 128
        S.dma("sp", qj, qj.t[:], QTv[:, :, tsl], reads=[rq])
        S.dma("sp", sgj, sgj.t[:], SGv[:, :, tsl], reads=[rs])
        for n in range(4):
            tiles = [(g, r) for g in range(j + 1) for r in range(8)]
            tb = {}

            def at_s0(i, n=n):
                g, r = tiles[i]
                if r == 0:
                    ensure_piece(j, n, g)
                    for ahead in (1, 2):
                        gg = g + ahead
                        if gg <= j:
                            ensure_piece(j, n, gg)
                        else:
                            nx = (j, n + 1) if n < 3 else ((j + 1, 0) if j + 1 < NBL else None)
                            if nx is not None and gg - (j + 1) <= nx[0]:
                                ensure_piece(nx[0], nx[1], gg - (j + 1))
                kbuf, vbuf = kvpre[(j, n, g)]
                kb = 8 * g + r
                st = _rr(W, st_["wc"])
                st_["wc"] += 1
                pt = _rr(pts, i)
                pm = _rr(pms, i)
                tb[i] = (pm, vbuf)
                S.op("pe", lambda: nc.tensor.matmul(st.t[:], kbuf.t[:, r, :], qj.t[:, 4 * n:4 * n + 4, :], start=True, stop=True), reads=[kbuf, qj], writes=[st])
                S.op("act", lambda: nc.scalar.activation(pt.t[:], st.t[:].rearrange("p (h t) -> p h t", h=4), AF.Exp), reads=[st], writes=[pt])
                S.op("dve", lambda: nc.vector.tensor_tensor(
                    pm.t[:], pt.t[:], MT.t[:, kb * 128:(kb + 1) * 128].unsqueeze(1).to_broadcast([128, 4, 128]), op=ALU.mult),
                    reads=[pt, MT], writes=[pm])

            def at_s1(i):
                g, r = tiles[i]
                kb = 8 * g + r
                pm, vbuf = tb.pop(i)
                S.op("pe", lambda: nc.tensor.matmul(OP.t[:], vbuf.t[:, r, :], pm.t[:].rearrange("p h t -> p (h t)"),
                                                    start=(kb == 0), stop=(kb == nkb - 1)), reads=[vbuf, pm], writes=[OP])
                S.op("pe", lambda: nc.tensor.matmul(DEN.t[:], ones.t[:], pm.t[:].rearrange("p h t -> p (h t)"),
                                                    start=(kb == 0), stop=(kb == nkb - 1)), reads=[ones, pm], writes=[DEN])

            _pipeline(len(tiles), [(0, at_s0), (2, at_s1)])
            S.op("dve", lambda: nc.vector.reciprocal(rden.t[:], DEN.t[:]), reads=[DEN], writes=[rden])
            S.op("dve", lambda: nc.vector.tensor_tensor(onrm.t[:], OP.t[:].rearrange("p (h t) -> p h t", h=4), rden.t[:].rearrange("p (h t) -> p h t", h=4), op=ALU.mult),
                 reads=[OP, rden], writes=[onrm])
            S.op("dve", lambda n=n: nc.vector.tensor_tensor(ogt.t[:], onrm.t[:], sgj.t[:, 4 * n:4 * n + 4, :], op=ALU.mult), reads=[onrm, sgj], writes=[ogt])
            S.dma("sp", rOGT, OGTv[:, 4 * n:4 * n + 4, tsl], ogt.t[:], reads=[ogt])

    prep_idx(0)
    idx(0)
    for j in range(NBL):
        if j + 1 < NBL:
            prep_idx(j + 1)
            idx(j + 1)
        bis(j)
        att(j)
    S.end_phase()


def _bf(a):
    return np.asarray(a, dtype=np.float32).astype(ml_dtypes.bfloat16)


def host_consts():
    p = np.arange(128)
    ropec = np.zeros((128, 4), np.float32)
    inv_qk = (np.float32(1.0) / np.power(np.float32(THETA), np.arange(16, dtype=np.float32) / np.float32(16))).astype(np.float32)
    inv_i = (np.float32(1.0) / np.power(np.float32(THETA), np.arange(8, dtype=np.float32) / np.float32(8))).astype(np.float32)
    for q in range(128):
        if q < 32:
            ropec[q, 0] = inv_qk[q % 16]
            ropec[q, 1] = -1.0 if q < 16 else 1.0
        r = q % 64
        if r < 16:
            ropec[q, 2] = inv_i[r % 8]
            ropec[q, 3] = -1.0 if r < 8 else 1.0
    pqk = np.zeros((128, 128), np.float32)
    pii = np.zeros((128, 128), np.float32)
    for m in range(128):
        pm = m + 16 if m < 16 else (m - 16 if m < 32 else m)
        pqk[pm, m] = 1.0
        r = m % 64
        pm = m + 8 if r < 8 else (m - 8 if r < 16 else m)
        pii[pm, m] = 1.0
    ident = np.eye(128, dtype=np.float32)
    utri = (p[:, None] >= p[None, :]).astype(np.float32)
    return {"ropec": ropec, "perm_qk": _bf(pqk), "perm_i": _bf(pii), "ident": _bf(ident),
            "ones_bf": _bf(np.ones((128, 128))), "ones_f": np.ones((128, 128), np.float32), "utri": _bf(utri)}


def host_tokens(c):
    return np.concatenate([np.arange(128 * blk(c, j), 128 * blk(c, j) + 128) for j in range(NBL)])


def host_wA(w_in_a):
    w = w_in_a[0]
    q, k, v, gate, iq, ik, iw = np.split(w, [2048, 2560, 3072, 5120, 6144, 6208], axis=1)
    return np.ascontiguousarray(np.concatenate([k, ik, ik, v, q, gate, iq, iw], axis=1))


def emit_P(S, nc, D, layer, with_proj):
    L = str(layer)
    S.begin_phase("P%s_" % L)
    OGTd, xT, XO = D["OGT" + L], D["xin" + L], D["xout" + L]
    rog, rxin, rXO = S.dram(OGTd, "OGT" + L), S.dram(xT, "xin" + L), S.dram(XO, "xout" + L)
    wo, wg, wple, pT = D["wo" + L], D["wg" + L], D["wple" + L], D["pT" + L]
    ogt = S.sbuf("ogt", [128, 16, TOK], BF16)
    yT = S.sbuf("yT", [128, 16, TOK], F32)
    wbufs = [S.sbuf("wb%d" % i, [128, 16, 512], BF16) for i in range(2)]
    ptb = S.sbuf("ptb", [128, 2, TOK], BF16)
    wpl = S.sbuf("wpl", [128, 2, DM], BF16)
    onesf = S.sbuf("onesf", [128, 128], F32)
    lng = S.sbuf("lng", [128, 16], F32)
    lnb = S.sbuf("lnb", [128, 16], F32)
    xin = [S.sbuf("xin%d" % i, [128, 512], F32) for i in range(2)]
    sq = [S.sbuf("sq%d" % i, [128, 512], F32) for i in range(2)]
    meanb = S.sbuf("meanb", [128, 512], F32)
    rstdb = S.sbuf("rstdb", [128, 512], F32)
    tmpf = [S.sbuf("tmpf%d" % i, [128, 512], F32) for i in range(2)]
    stg = [S.sbuf("stg%d" % i, [128, 512], BF16) for i in range(4)]
    psb = [S.psum("ps%d" % i, [128, 512], F32) for i in range(4)]
    SUM = S.psum("SUM", [128, 512], F32)
    SSQ = S.psum("SSQ", [128, 512], F32)
    S.dma("sp", ogt, ogt.t[:], OGTd.rearrange("(h d) t -> d h t", d=128), reads=[rog])
    S.dma("sp", onesf, onesf.t[:], D["ones_f"])
    S.dma("sp", lng, lng.t[:], D["lng" + L])
    S.dma("sp", lnb, lnb.t[:], D["lnb" + L])
    S.dma("pool", ptb, ptb.t[:], pT.rearrange("(c p) t -> p c t", p=128))
    S.dma("pool", wpl, wpl.t[:], wple.rearrange("(c p) n -> p c n", p=128))
    t1, t2 = tmpf
    cnt = 0
    for g in range(4):
        wb = _rr(wbufs, g)
        S.dma("pool", wb, wb.t[:], wo[:, 512 * g:512 * g + 512].rearrange("(c p) n -> p c n", p=128))
        for mi in range(4):
            m = 4 * g + mi
            for hh in range(2):
                sl = slice(hh * 512, (hh + 1) * 512)
                ps = _rr(psb, cnt)
                xi = _rr(xin, cnt)
                cnt += 1

                def mm(ps=ps, wb=wb, mi=mi, sl=sl):
                    last = None
                    for k in range(16):
                        last = nc.tensor.matmul(ps.t[:], wb.t[:, k, mi * 128:(mi + 1) * 128], ogt.t[:, k, sl], start=(k == 0), stop=(k == 15))
                    return last
                S.op("pe", mm, reads=[wb, ogt], writes=[ps])
                S.dma("sp", xi, xi.t[:], xT[m * 128:(m + 1) * 128, sl], reads=[rxin])
                S.op("dve", lambda ps=ps, xi=xi, m=m, sl=sl: nc.vector.scalar_tensor_tensor(yT.t[:, m, sl], xi.t[:], ALPHA, ps.t[:], op0=ALU.mult, op1=ALU.add),
                     reads=[xi, ps], writes=[yT])
    for hh in range(2):
        sl = slice(hh * 512, (hh + 1) * 512)
        for m in range(16):
            s_ = _rr(sq, m)
            S.op("pe", lambda m=m, sl=sl: nc.tensor.matmul(SUM.t[:], onesf.t[:], yT.t[:, m, sl], start=(m == 0), stop=(m == 15)), reads=[onesf, yT], writes=[SUM])
            S.op("act", lambda m=m, sl=sl, s_=s_: nc.scalar.activation(s_.t[:], yT.t[:, m, sl], AF.Square), reads=[yT], writes=[s_])
            S.op("pe", lambda m=m, s_=s_: nc.tensor.matmul(SSQ.t[:], onesf.t[:], s_.t[:], start=(m == 0), stop=(m == 15)), reads=[onesf, s_], writes=[SSQ])
        S.op("act", lambda: nc.scalar.activation(meanb.t[:], SUM.t[:], AF.Copy, scale=1.0 / DM), reads=[SUM], writes=[meanb])
        S.op("act", lambda: nc.scalar.activation(t1.t[:], SSQ.t[:], AF.Copy, scale=1.0 / DM), reads=[SSQ], writes=[t1])
        S.op("dve", lambda: nc.vector.tensor_tensor(t2.t[:], meanb.t[:], meanb.t[:], op=ALU.mult), reads=[meanb], writes=[t2])
        S.op("dve", lambda: nc.vector.tensor_tensor(t1.t[:], t1.t[:], t2.t[:], op=ALU.subtract), reads=[t1, t2], writes=[t1])
        S.op("dve", lambda: nc.vector.tensor_scalar(t1.t[:], t1.t[:], LN_EPS, None, op0=ALU.add), reads=[t1], writes=[t1])
        S.op("act", lambda: nc.scalar.activation(t2.t[:], t1.t[:], AF.Sqrt), reads=[t1], writes=[t2])
        S.op("dve", lambda: nc.vector.reciprocal(rstdb.t[:], t2.t[:]), reads=[t2], writes=[rstdb])
        for m in range(16):
            S.op("dve", lambda m=m, sl=sl: nc.vector.tensor_tensor(t1.t[:], yT.t[:, m, sl], meanb.t[:], op=ALU.subtract), reads=[yT, meanb], writes=[t1])
            S.op("dve", lambda: nc.vector.tensor_tensor(t2.t[:], t1.t[:], rstdb.t[:], op=ALU.mult), reads=[t1, rstdb], writes=[t2])
            S.op("dve", lambda m=m, sl=sl: nc.vector.tensor_scalar(yT.t[:, m, sl], t2.t[:], lng.t[:, m:m + 1], lnb.t[:, m:m + 1], op0=ALU.mult, op1=ALU.add),
                 reads=[t2, lng, lnb], writes=[yT])
            S.op("act", lambda m=m, sl=sl: nc.scalar.activation(ogt.t[:, m, sl], yT.t[:, m, sl], AF.Copy), reads=[yT], writes=[ogt])
    xlb = ogt
    for g in range(4):
        wb = _rr(wbufs, g)
        S.dma("pool", wb, wb.t[:], wg[:, 512 * g:512 * g + 512].rearrange("(c p) n -> p c n", p=128))
        for mi in range(4):
            m = 4 * g + mi
            for hh in range(2):
                sl = slice(hh * 512, (hh + 1) * 512)
                ps = _rr(psb, cnt)
                ps2 = _rr(psb, cnt + 1)
                cnt += 2

                def mm(ps=ps, wb=wb, mi=mi, sl=sl):
                    last = None
                    for k in range(16):
                        last = nc.tensor.matmul(ps.t[:], wb.t[:, k, mi * 128:(mi + 1) * 128], xlb.t[:, k, sl], start=(k == 0), stop=(k == 15))
                    return last
                S.op("pe", mm, reads=[wb, xlb], writes=[ps])
                S.op("act", lambda ps=ps: nc.scalar.activation(t1.t[:], ps.t[:], AF.Sigmoid), reads=[ps], writes=[t1])

                def mm2(ps2=ps2, m=m, sl=sl):
                    last = None
                    for k in range(2):
                        last = nc.tensor.matmul(ps2.t[:], wpl.t[:, k, m * 128:(m + 1) * 128], ptb.t[:, k, sl], start=(k == 0), stop=(k == 1))
                    return last
                S.op("pe", mm2, reads=[wpl, ptb], writes=[ps2])
                S.op("dve", lambda ps2=ps2: nc.vector.tensor_tensor(t2.t[:], ps2.t[:], t1.t[:], op=ALU.mult), reads=[ps2, t1], writes=[t2])
                S.op("dve", lambda m=m, sl=sl: nc.vector.tensor_tensor(yT.t[:, m, sl], yT.t[:, m, sl], t2.t[:], op=ALU.add), reads=[yT, t2], writes=[yT])
    S.dma("sp", rXO, XO.rearrange("(m p) t -> p m t", p=128), yT.t[:], reads=[yT])
    if with_proj:
        for m in range(16):
            S.op("act", lambda m=m: nc.scalar.activation(ogt.t[:, m, :], yT.t[:, m, :], AF.Copy), reads=[yT], writes=[ogt])
        QT, KT, SG, V = D["QT1"], D["KT1"], D["SG1"], D["V1"]
        rq, rk, rs, rv = [S.dram(a, n) for a, n in ((QT, "QT1"), (KT, "KT1"), (SG, "SG1"), (V, "V1"))]
        chunks1 = [("k1", KT[n * 128:(n + 1) * 128, :], rk) for n in range(4)]
        emit_proj(S, nc, ogt, D["wB"], chunks1, (512, V, rv, BF16), None, None, wbufs, stg, tmpf, psb)
        for src, dst in (("KT1", "KTG1"), ("V1", "VG1")):
            S.collective("AllGather", S.dram(D[dst], dst), D[dst], D[src], reads=[S.dram(D[src], src)])
        chunks2 = [("q1", QT[h * 128:(h + 1) * 128, :], rq) for h in range(16)] + [("g", SG[h * 128:(h + 1) * 128, :], rs) for h in range(16)]
        emit_proj(S, nc, ogt, D["wB"], chunks2, None, None, None, wbufs, stg, tmpf, psb, col_base=1024)
    S.end_phase()


def emit_C(S, nc, D):
    S.begin_phase("C_")
    QT, SG, OGT = D["QT1"], D["SG1"], D["OGT1"]
    rq, rs = S.dram(QT, "QT1"), S.dram(SG, "SG1")
    rkg, rvg = S.dram(D["KTG1"], "KTG1"), S.dram(D["VG1"], "VG1")
    KTGv, VGv = _kv_views(D["KTG1"], D["VG1"])
    QTv = QT.rearrange("(h d) t -> d h t", d=128)
    SGv = SG.rearrange("(h d) t -> d h t", d=128)
    OGTv = OGT.rearrange("(h d) t -> d h t", d=128)
    rOGT = S.dram(OGT, "OGT1")
    utri = S.sbuf("utri", [128, 128], BF16)
    ones = S.sbuf("ones", [128, 128], BF16)
    S.dma("sp", utri, utri.t[:], D["utri"])
    S.dma("sp", ones, ones.t[:], D["ones_bf"])
    qj = S.sbuf("qj", [128, 16, 128], BF16)
    sgj = S.sbuf("sgj", [128, 16, 128], BF16)
    cmj = S.sbuf("cmj", [128, 8, 128], BF16)
    es = [S.sbuf("e%d" % i, [128, 4, 128], F32) for i in range(5)]
    Ls = [S.sbuf("L%d" % i, [128, 4, 128], BF16) for i in range(4)]
    gxs = [S.sbuf("gx%d" % i, [128, 4, 128], F32) for i in range(2)]
    As = [S.sbuf("A%d" % i, [128, 4, 128], BF16) for i in range(4)]
    Lsums = [S.sbuf("Lsum%d" % i, [128, 4, 128], BF16) for i in range(3)]
    kbufs = [S.sbuf("kbuf%d" % i, [128, 8, 128], BF16) for i in range(4)]
    vbufs = [S.sbuf("vbuf%d" % i, [128, 8, 128], BF16) for i in range(4)]
    ogt = S.sbuf("ogt", [128, 4, 128], BF16)
    Z = [S.psum("Z%d" % i, [128, 512], F32) for i in range(3)]
    E = [S.psum("E%d" % i, [128, 512], F32) for i in range(3)]
    OP = S.psum("OP", [128, 512], F32)
    st_ = {"kvc": 0}
    kvpre = {}

    def ensure_piece(jj, nn, gg):
        if (jj, nn, gg) in kvpre:
            return
        kbuf = _rr(kbufs, st_["kvc"])
        vbuf = _rr(vbufs, st_["kvc"])
        st_["kvc"] += 1
        kvpre[(jj, nn, gg)] = (kbuf, vbuf)
        S.dma("sp", kbuf, kbuf.t[:], KTGv[:, nn, :, gg * 128:(gg + 1) * 128].rearrange("c d t -> d c t"), reads=[rkg])
        S.dma("sp", vbuf, vbuf.t[:], VGv[:, gg, :, nn, :].rearrange("c p d -> p c d"), reads=[rvg])

    for j in range(NBL):
        tsl = slice(j * 128, (j + 1) * 128)
        S.dma("sp", qj, qj.t[:], QTv[:, :, tsl], reads=[rq])
        S.dma("sp", sgj, sgj.t[:], SGv[:, :, tsl], reads=[rs])
        S.dma("sp", cmj, cmj.t[:], D["CM"][:, j, :, :])
        for n in range(4):
            tiles = []
            for g in range(j, -1, -1):
                order = list(range(7, -1, -1)) if g % 2 == 0 else list(range(8))
                for r in order:
                    tiles.append((g, r))
            NT = len(tiles)
            tb = {}

            def c_s0(i, n=n):
                g, r = tiles[i]
                if i % 8 == 0:
                    ensure_piece(j, n, g)
                    for ahead in (1, 2):
                        gg = g - ahead
                        if gg >= 0:
                            ensure_piece(j, n, gg)
                        else:
                            nx = (j, n + 1) if n < 3 else ((j + 1, 0) if j + 1 < NBL else None)
                            if nx is not None and nx[0] + gg + 1 >= 0:
                                ensure_piece(nx[0], nx[1], nx[0] + gg + 1)
                kbuf, vbuf = kvpre[(j, n, g)]
                z = _rr(Z, i)
                e = _rr(es, i)
                L = _rr(Ls, i)
                tb[i] = {"e": e, "L": L, "vbuf": vbuf}
                S.op("pe", lambda: nc.tensor.matmul(z.t[:], kbuf.t[:, r, :], qj.t[:, 4 * n:4 * n + 4, :], start=True, stop=True), reads=[kbuf, qj], writes=[z])
                S.op("act", lambda: nc.scalar.activation(e.t[:], z.t[:].rearrange("p (h t) -> p h t", h=4), AF.Exp), reads=[z], writes=[e])
                S.op("act", lambda: nc.scalar.activation(L.t[:], e.t[:], AF.Ln, bias=1.0), reads=[e], writes=[L])
                if g == j:
                    cmb = cmj.t[:, r, :].unsqueeze(1).to_broadcast([128, 4, 128])
                    S.op("dve", lambda: nc.vector.tensor_tensor(L.t[:], L.t[:], cmb, op=ALU.mult), reads=[L, cmj], writes=[L])

            def c_s1(i):
                g, r = tiles[i]
                first = (i == 0)
                t_ = tb[i]
                L = t_["L"]
                Eb = _rr(E, i)
                t_["Eb"] = Eb
                Lprev = _rr(Lsums, i - 1)
                Lcur = _rr(Lsums, i)
                S.op("pe", lambda: nc.tensor.matmul(Eb.t[:], utri.t[:], L.t[:].rearrange("p h t -> p (h t)"), start=True, stop=first), reads=[utri, L], writes=[Eb])
                if not first:
                    S.op("pe", lambda: nc.tensor.matmul(Eb.t[:], ones.t[:], Lprev.t[:].rearrange("p h t -> p (h t)"), start=False, stop=True),
                         reads=[ones, Lprev], writes=[Eb])
                if i < NT - 1:
                    if first:
                        S.op("dve", lambda: nc.vector.tensor_copy(Lcur.t[:], L.t[:]), reads=[L], writes=[Lcur])
                    else:
                        S.op("dve", lambda: nc.vector.tensor_tensor(Lcur.t[:], Lprev.t[:], L.t[:], op=ALU.add), reads=[Lprev, L], writes=[Lcur])

            def c_s1b(i):
                g, r = tiles[i]
                t_ = tb[i]
                e, Eb = t_["e"], t_["Eb"]
                gx = _rr(gxs, i)
                A = _rr(As, i)
                t_["A"] = A
                S.op("act", lambda: nc.scalar.activation(gx.t[:], Eb.t[:].rearrange("p (h t) -> p h t", h=4), AF.Exp, scale=-1.0), reads=[Eb], writes=[gx])
                S.op("dve", lambda: nc.vector.tensor_tensor(A.t[:], e.t[:], gx.t[:], op=ALU.mult), reads=[e, gx], writes=[A])
                if g == j:
                    cmb = cmj.t[:, r, :].unsqueeze(1).to_broadcast([128, 4, 128])
                    S.op("dve", lambda: nc.vector.tensor_tensor(A.t[:], A.t[:], cmb, op=ALU.mult), reads=[A, cmj], writes=[A])

            def c_s2(i):
                g, r = tiles[i]
                t_ = tb.pop(i)
                A, vbuf = t_["A"], t_["vbuf"]
                S.op("pe", lambda: nc.tensor.matmul(OP.t[:], vbuf.t[:, r, :], A.t[:].rearrange("p h t -> p (h t)"), start=(i == 0), stop=(i == NT - 1)),
                     reads=[vbuf, A], writes=[OP])

            _pipeline(NT, [(0, c_s0), (2, c_s1), (3, c_s1b), (5, c_s2)])
            S.op("dve", lambda n=n: nc.vector.tensor_tensor(ogt.t[:], OP.t[:].rearrange("p (h t) -> p h t", h=4), sgj.t[:, 4 * n:4 * n + 4, :], op=ALU.mult),
                 reads=[OP, sgj], writes=[ogt])
            S.dma("sp", rOGT, OGTv[:, 4 * n:4 * n + 4, tsl], ogt.t[:], reads=[ogt])
    S.end_phase()


def host_pen(c):
    p = np.arange(128)[:, None, None, None]
    sl = np.arange(8)[None, None, :, None]
    s = np.arange(128)[None, None, None, :]
    bq = np.array([blk(c, jj) for jj in range(NBL)])[None, :, None, None]
    bk = np.array([[blk(cc, jj) for cc in range(8)] for jj in range(NBL)])[None, :, :, None]
    pen = np.where(128 * bk + s <= 128 * bq + p, 0.0, NEG).astype(np.float32)
    return np.ascontiguousarray(pen.reshape(128, NBL, 1024))


def host_cm(c):
    s = np.arange(128)[:, None, None, None]
    t = np.arange(128)[None, None, None, :]
    bq = np.array([blk(c, jj) for jj in range(NBL)])[None, :, None, None]
    bk = np.array([[blk(cc, jj) for cc in range(8)] for jj in range(NBL)])[None, :, :, None]
    return _bf((128 * bk + s < 128 * bq + t).astype(np.float32))


def build_fused():
    nc = bass.Bass("TRN2", target_bir_lowering=False)
    D = {}

    def ein(name, shape, dt):
        D[name] = nc.dram_tensor(name, shape, dt, kind="ExternalInput").ap()

    def internal(name, shape, dt):
        D[name] = nc.dram_tensor(name, shape, dt, kind="Internal").ap()

    ein("xT", [DM, TOK], F32)
    ein("pos", [1, TOK], I32)
    ein("wA", [DM, 45 * 128 + 512 + 16], F32)
    ein("wB", [DM, 5120], F32)
    ein("ropec", [128, 4], F32)
    for nm in ("perm_qk", "perm_i", "ident", "ones_bf", "utri"):
        ein(nm, [128, 128], BF16)
    ein("ones_f", [128, 128], F32)
    ein("PEN", [128, NBL, 1024], F32)
    ein("CM", [128, NBL, 8, 128], BF16)
    for L in ("0", "1"):
        ein("wo" + L, [DM, DM], F32)
        ein("wg" + L, [DM, DM], F32)
        ein("wple" + L, [256, DM], F32)
        ein("pT" + L, [256, TOK], F32)
        ein("lng" + L, [128, 16], F32)
        ein("lnb" + L, [128, 16], F32)
    for L in ("0", "1"):
        internal("QT" + L, [2048, TOK], BF16)
        internal("KT" + L, [512, TOK], BF16)
        internal("SG" + L, [2048, TOK], BF16)
        internal("V" + L, [TOK, 512], BF16)
        internal("KTG" + L, [NCORES * 512, TOK], BF16)
        internal("VG" + L, [NCORES * TOK, 512], BF16)
        internal("OGT" + L, [2048, TOK], BF16)
    internal("IQT0", [1024, TOK], BF16)
    internal("IK0", [128, TOK], BF16)
    internal("IKG0", [NCORES * 128, TOK], BF16)
    internal("IW0", [TOK, 16], F32)
    internal("X1", [DM, TOK], F32)
    D["OUT"] = nc.dram_tensor("OUT", [DM, TOK], F32, kind="ExternalOutput").ap()
    D["xin0"], D["xout0"], D["xin1"], D["xout1"] = D["xT"], D["X1"], D["X1"], D["OUT"]
    S = Sched(nc)
    emit_A(S, nc, D)
    emit_B(S, nc, D)
    emit_P(S, nc, D, 0, True)
    emit_C(S, nc, D)
    emit_P(S, nc, D, 1, False)
    S.finish()
    return nc


def kernel(**inputs):
    cs = host_consts()
    x = np.asarray(inputs["x"])[0]
    pos = np.asarray(inputs["positions"]).astype(np.int32)
    wA = host_wA(np.asarray(inputs["w_in_a"]))
    wib = np.asarray(inputs["w_in_b"])[0]
    wkv = np.asarray(inputs["w_kv_b"])
    wB = np.ascontiguousarray(np.concatenate([wkv[:, :512], wkv[:, 512:], wib[:, :2048], wib[:, 2048:]], axis=1))
    p = np.asarray(inputs["p"])
    shared = {"wA": wA, "wB": wB, "ropec": cs["ropec"], "perm_qk": cs["perm_qk"], "perm_i": cs["perm_i"], "ident": cs["ident"],
              "ones_bf": cs["ones_bf"], "utri": cs["utri"], "ones_f": cs["ones_f"]}
    for L, wo in ((0, "w_o_a"), (1, "w_o_b")):
        shared["wo%d" % L] = np.ascontiguousarray(np.asarray(inputs[wo])[0])
        shared["wg%d" % L] = np.ascontiguousarray(np.asarray(inputs["w_ple_gate"])[L])
        shared["wple%d" % L] = np.ascontiguousarray(np.asarray(inputs["w_ple"])[L])
        shared["lng%d" % L] = np.ascontiguousarray(np.asarray(inputs["ln_g"])[L].reshape(16, 128).T)
        shared["lnb%d" % L] = np.ascontiguousarray(np.asarray(inputs["ln_b"])[L].reshape(16, 128).T)
    ims = []
    for c in range(NCORES):
        tok = host_tokens(c)
        im = dict(shared)
        im["xT"] = np.ascontiguousarray(x[tok].T)
        im["pos"] = np.ascontiguousarray(pos[:, tok])
        im["PEN"] = host_pen(c)
        im["CM"] = host_cm(c)
        for L in (0, 1):
            im["pT%d" % L] = np.ascontiguousarray(p[L, 0][tok].T)
        ims.append(im)
    res = run_bass_kernel_spmd(build_fused(), ims, core_ids=list(range(NCORES))).results
    out = np.zeros((1, SEQ, DM), np.float32)
    for c in range(NCORES):
        out[0, host_tokens(c), :] = np.asarray(res[c]["OUT"]).T
    return out
```
